# Optimizing a Trainium2 kernel written in Bass

```python
import math
import jax, jax.numpy as jnp
from jax import lax
import numpy as np

D_MODEL = 1024
BATCH = 4
SEQ = 4096
DEPTH = 4
DEC_BATCH = 32
DEC_SEQ = 8
PAST_LEN = 8192
PAGE_SIZE = 128

HEAD_DIM = 64
MIX_W = D_MODEL
D_A = MIX_W // 4
H_A = D_A // HEAD_DIM
D_B = MIX_W // 2
H_B = D_B // HEAD_DIM
SSM_GROUPS = 2
N_STATE = 128
CONV_W = 4
CONV_DIM = D_B + 2 * SSM_GROUPS * N_STATE
SSD_CHUNK = 128
D_C = MIX_W - D_A - D_B
SCONV_W = 3
D_FF = 4 * D_MODEL
DILATED_CONFIGS = ((128, 1), (512, 4), (2048, 16))
WIN_MAX = 2048
IN_COLS = 3 * D_A + D_B + CONV_DIM + H_B + 3 * D_C
EPS = 1e-6

kernel_name = "hymba_dilated_ssd_shortconv_decoder_step"


def rms_norm(x, g):
    x32 = x.astype(jnp.float32)
    y = x32 * lax.rsqrt(jnp.mean(x32 * x32, axis=-1, keepdims=True) + EPS)
    return (y * g.astype(jnp.float32)).astype(x.dtype)


def alibi_slopes():
    return 2.0 ** (-8.0 * jnp.arange(1, H_A + 1, dtype=jnp.float32) / H_A)


def causal_conv(u, prev, w):
    width = w.shape[0]
    L = u.shape[1]
    ext = jnp.concatenate([prev.astype(u.dtype), u], axis=1)
    out = ext[:, 0:L] * w[0]
    for k in range(1, width):
        out = out + ext[:, k:k + L] * w[k]
    return out, ext[:, L:]


def dilated_branch_prompt(q, k, v, win, dil, slopes):
    Bsz, T, H, E = q.shape
    L = T // dil
    span = win // dil
    blk = span
    Lp = -(-L // blk) * blk
    nb = Lp // blk

    def to_strided(t):
        t = t.reshape(Bsz, L, dil, H, E).transpose(0, 2, 1, 3, 4)
        t = jnp.pad(t, ((0, 0), (0, 0), (0, Lp - L), (0, 0), (0, 0)))
        return t.reshape(Bsz, dil, nb, blk, H, E)

    qs, ks, vs = to_strided(q), to_strided(k), to_strided(v)
    pad = ((0, 0), (0, 0), (1, 0), (0, 0), (0, 0), (0, 0))
    k_ext, v_ext = jnp.pad(ks, pad), jnp.pad(vs, pad)
    k_band = jnp.concatenate([k_ext[:, :, :-1], k_ext[:, :, 1:]], axis=3)
    v_band = jnp.concatenate([v_ext[:, :, :-1], v_ext[:, :, 1:]], axis=3)

    s = jnp.einsum('bdnqhe,bdnkhe->bdnhqk', qs, k_band) * (HEAD_DIM ** -0.5)
    qi = jnp.arange(blk)
    ki = jnp.arange(2 * blk)
    diff = blk + qi[:, None] - ki[None, :]
    key_valid = (jnp.arange(nb)[:, None, None] > 0) | (ki[None, None, :] >= blk)
    mask = (diff >= 0)[None] & (diff <= span)[None] & key_valid
    dist = (dil * diff).astype(jnp.float32)
    s = s - slopes[:, None, None] * dist
    s = jnp.where(mask[None, None, :, None], s, -jnp.inf)
    m = jnp.max(s, axis=-1, keepdims=True)
    p = jnp.exp(s - m)
    l = jnp.sum(p, axis=-1)
    o = jnp.einsum('bdnhqk,bdnkhe->bdnqhe', p, v_band)
    o = o / jnp.transpose(l, (0, 1, 2, 4, 3))[..., None]

    o = o.reshape(Bsz, dil, Lp, H, E)[:, :, :L].transpose(0, 2, 1, 3, 4).reshape(Bsz, T, H, E)

    def stat_back(t):
        t = jnp.transpose(t, (0, 1, 2, 4, 3)).reshape(Bsz, dil, Lp, H)[:, :, :L]
        return t.transpose(0, 2, 1, 3).reshape(Bsz, T, H)

    return o, stat_back(m[..., 0]), stat_back(l)


def dilated_branch_sample(q, k_all, v_all, win, dil, slopes, wb):
    S = q.shape[1]
    span = win // dil
    j = jnp.arange(span + 1)
    idx = wb + jnp.arange(S)[:, None] - dil * j[None, :]
    valid = idx >= 0
    idx_c = jnp.maximum(idx, 0)
    kg = k_all[:, idx_c]
    vg = v_all[:, idx_c]
    s = jnp.einsum('bshe,bsjhe->bhsj', q, kg) * (HEAD_DIM ** -0.5)
    s = s - slopes[:, None, None] * (dil * j).astype(jnp.float32)[None, None, :]
    s = jnp.where(valid[None, None], s, -jnp.inf)
    m = jnp.max(s, axis=-1, keepdims=True)
    p = jnp.exp(s - m)
    l = jnp.sum(p, axis=-1)
    o = jnp.einsum('bhsj,bsjhe->bshe', p, vg)
    o = o / jnp.transpose(l, (0, 2, 1))[..., None]
    return o, jnp.transpose(m[..., 0], (0, 2, 1)), jnp.transpose(l, (0, 2, 1))


def merge_branches(results):
    os_ = jnp.stack([r[0] for r in results])
    ms = jnp.stack([r[1] for r in results])
    ls = jnp.stack([r[2] for r in results])
    m_all = jnp.max(ms, axis=0, keepdims=True)
    wts = ls * jnp.exp(ms - m_all)
    return jnp.sum(wts[..., None] * os_, axis=0) / jnp.sum(wts, axis=0)[..., None]


def ssd_scan(x, dt, a, bm, cm, h0, chunk):
    Bsz, L, H, P = x.shape
    G, N = bm.shape[2], bm.shape[3]
    nc = L // chunk
    f32 = jnp.float32
    x = x.astype(f32).reshape(Bsz, nc, chunk, H, P)
    dt = dt.reshape(Bsz, nc, chunk, H)
    bh = jnp.repeat(bm.astype(f32), H // G, axis=2).reshape(Bsz, nc, chunk, H, N)
    ch = jnp.repeat(cm.astype(f32), H // G, axis=2).reshape(Bsz, nc, chunk, H, N)
    acs = jnp.cumsum(dt * a, axis=2)
    causal = jnp.tril(jnp.ones((chunk, chunk), dtype=bool))
    seg = acs[:, :, :, None, :] - acs[:, :, None, :, :]
    decay = jnp.exp(jnp.where(causal[None, None, :, :, None], seg, -jnp.inf))
    scores = jnp.einsum('bclhn,bcshn->bclsh', ch, bh) * decay
    y_diag = jnp.einsum('bclsh,bcshp->bclhp', scores * dt[:, :, None, :, :], x)
    to_end = jnp.exp(acs[:, :, -1:, :] - acs)
    states = jnp.einsum('bclhn,bclhp->bchpn', bh * (to_end * dt)[..., None], x)
    chunk_decay = jnp.exp(acs[:, :, -1, :])

    def step(h, inp):
        dec, st = inp
        return dec[:, :, None, None] * h + st, h

    h_final, h_start = lax.scan(step, h0.astype(f32),
                                (jnp.moveaxis(chunk_decay, 1, 0), jnp.moveaxis(states, 1, 0)))
    h_start = jnp.moveaxis(h_start, 0, 1)
    y_off = jnp.einsum('bclhn,bchpn->bclhp', ch * jnp.exp(acs)[..., None], h_start)
    return (y_diag + y_off).reshape(Bsz, L, H, P), h_final


def token_mixer(h, w_in, w_out, attn_norm, conv_w, conv_b, a_log, dt_bias, d_skip, ssm_norm,
                sconv_w, sconv_norm, prev, prompt):
    Bsz, L, _ = h.shape
    dtype = h.dtype
    proj = h @ w_in
    cuts = list(np.cumsum([D_A, D_A, D_A, D_B, CONV_DIM, H_B, D_C, D_C]))
    q, k, v, z, xbc, dt_raw, b_gate, c_gate, u_c = jnp.split(proj, [int(c) for c in cuts], axis=-1)
    q = q.reshape(Bsz, L, H_A, HEAD_DIM).astype(jnp.float32)
    k = k.reshape(Bsz, L, H_A, HEAD_DIM)
    v = v.reshape(Bsz, L, H_A, HEAD_DIM)
    slopes = alibi_slopes()

    if prompt:
        res = [dilated_branch_prompt(q, k.astype(jnp.float32), v.astype(jnp.float32), w, d, slopes)
               for (w, d) in DILATED_CONFIGS]
        wb = min(WIN_MAX, L)
        new_k, new_v = k[:, L - wb:], v[:, L - wb:]
        h0 = jnp.zeros((Bsz, H_B, HEAD_DIM, N_STATE), jnp.float32)
        conv_prev = jnp.zeros((Bsz, CONV_W - 1, CONV_DIM), dtype)
        sconv_prev = jnp.zeros((Bsz, SCONV_W - 1, D_C), dtype)
    else:
        buf_k, buf_v, h0, conv_prev, sconv_prev = prev
        wb = buf_k.shape[1]
        k_all = jnp.concatenate([buf_k.astype(dtype), k], axis=1)
        v_all = jnp.concatenate([buf_v.astype(dtype), v], axis=1)
        res = [dilated_branch_sample(q, k_all.astype(jnp.float32), v_all.astype(jnp.float32),
                                     w, d, slopes, wb) for (w, d) in DILATED_CONFIGS]
        new_k, new_v = k_all[:, L:], v_all[:, L:]
    o_a = rms_norm(merge_branches(res).astype(dtype).reshape(Bsz, L, D_A), attn_norm)

    xbc_c, new_conv = causal_conv(xbc, conv_prev, conv_w)
    xbc_c = jax.nn.silu(xbc_c + conv_b)
    x_s, bm, cm = jnp.split(xbc_c, [D_B, D_B + SSM_GROUPS * N_STATE], axis=-1)
    dt = jax.nn.softplus(dt_raw.astype(jnp.float32) + dt_bias.astype(jnp.float32))
    a = -jnp.exp(a_log.astype(jnp.float32))
    x4 = x_s.reshape(Bsz, L, H_B, HEAD_DIM)
    chunk = min(SSD_CHUNK, L) if prompt else L
    y, h_new = ssd_scan(x4, dt, a, bm.reshape(Bsz, L, SSM_GROUPS, N_STATE),
                        cm.reshape(Bsz, L, SSM_GROUPS, N_STATE), h0, chunk)
    y = y + d_skip.astype(jnp.float32)[:, None] * x4.astype(jnp.float32)
    y_b = rms_norm(y.astype(dtype).reshape(Bsz, L, D_B) * jax.nn.silu(z), ssm_norm)

    conv_out, new_sconv = causal_conv(c_gate * u_c, sconv_prev, sconv_w)
    y_c = rms_norm(b_gate * conv_out, sconv_norm)

    out = jnp.concatenate([o_a, y_b, y_c], axis=-1) @ w_out
    return out, (new_k, new_v, h_new, new_conv, new_sconv)


def run_trunk(x, states, weights, prompt):
    (norm_mix_pre, norm_mix_post, norm_mlp_pre, norm_mlp_post, w_in, w_out, attn_norm,
     ssm_conv_w, ssm_conv_b, ssm_a_log, ssm_dt_bias, ssm_d, ssm_norm, sconv_w, sconv_norm,
     w_mlp_up, w_mlp_down) = weights
    per_layer = []
    for l in range(DEPTH):
        prev = None if prompt else tuple(s[l] for s in states)
        h = rms_norm(x, norm_mix_pre[l])
        mix, new = token_mixer(h, w_in[l], w_out[l], attn_norm[l], ssm_conv_w[l], ssm_conv_b[l],
                               ssm_a_log[l], ssm_dt_bias[l], ssm_d[l], ssm_norm[l], sconv_w[l],
                               sconv_norm[l], prev, prompt)
        x = x + rms_norm(mix, norm_mix_post[l])
        h = rms_norm(x, norm_mlp_pre[l])
        f = jnp.square(jax.nn.relu(h @ w_mlp_up[l])) @ w_mlp_down[l]
        x = x + rms_norm(f, norm_mlp_post[l])
        per_layer.append(new)
    stacked = tuple(jnp.stack([ns[i] for ns in per_layer]) for i in range(5))
    return x, stacked


def setup_inputs(seed: int = 0) -> dict:
    key = jax.random.key(seed)
    ks = jax.random.split(key, 24)
    f32 = jnp.float32

    def nrm(k, shape, scale):
        return scale * jax.random.normal(k, shape, f32)

    def gain(k, shape):
        return 1.0 + 0.02 * jax.random.normal(k, shape, f32)

    win_buf = min(WIN_MAX, PAST_LEN)
    dt0 = jnp.exp(jax.random.uniform(ks[17], (DEPTH, H_B), f32, math.log(1e-3), math.log(1e-1)))
    return {
        "x_prompt": nrm(ks[0], (BATCH, SEQ, D_MODEL), 1.0),
        "x_sample": nrm(ks[1], (DEC_BATCH, DEC_SEQ, D_MODEL), 1.0),
        "cache_attn_k": nrm(ks[2], (DEPTH, DEC_BATCH, win_buf, H_A, HEAD_DIM), 1.0),
        "cache_attn_v": nrm(ks[3], (DEPTH, DEC_BATCH, win_buf, H_A, HEAD_DIM), 1.0),
        "state_ssm": nrm(ks[4], (DEPTH, DEC_BATCH, H_B, HEAD_DIM, N_STATE), 0.1),
        "state_ssm_conv": nrm(ks[5], (DEPTH, DEC_BATCH, CONV_W - 1, CONV_DIM), 1.0),
        "state_sconv": nrm(ks[6], (DEPTH, DEC_BATCH, SCONV_W - 1, D_C), 1.0),
        "norm_mix_pre": gain(ks[7], (DEPTH, D_MODEL)),
        "norm_mix_post": gain(ks[8], (DEPTH, D_MODEL)),
        "norm_mlp_pre": gain(ks[9], (DEPTH, D_MODEL)),
        "norm_mlp_post": gain(ks[10], (DEPTH, D_MODEL)),
        "w_in": nrm(ks[11], (DEPTH, D_MODEL, IN_COLS), D_MODEL ** -0.5),
        "w_out": nrm(ks[12], (DEPTH, MIX_W, D_MODEL), MIX_W ** -0.5),
        "attn_norm": gain(ks[13], (DEPTH, D_A)),
        "ssm_conv_w": nrm(ks[14], (DEPTH, CONV_W, CONV_DIM), CONV_W ** -0.5),
        "ssm_conv_b": nrm(ks[15], (DEPTH, CONV_DIM), 0.01),
        "ssm_a_log": jnp.log(jax.random.uniform(ks[16], (DEPTH, H_B), f32, 1.0, 16.0)),
        "ssm_dt_bias": dt0 + jnp.log(-jnp.expm1(-dt0)),
        "ssm_d": gain(ks[18], (DEPTH, H_B)),
        "ssm_norm": gain(ks[19], (DEPTH, D_B)),
        "sconv_w": nrm(ks[20], (DEPTH, SCONV_W, D_C), SCONV_W ** -0.5),
        "sconv_norm": gain(ks[21], (DEPTH, D_C)),
        "w_mlp_up": nrm(ks[22], (DEPTH, D_MODEL, D_FF), D_MODEL ** -0.5),
        "w_mlp_down": nrm(ks[23], (DEPTH, D_FF, D_MODEL), D_FF ** -0.5),
    }


def reference(x_prompt, x_sample, cache_attn_k, cache_attn_v, state_ssm, state_ssm_conv,
              state_sconv, norm_mix_pre, norm_mix_post, norm_mlp_pre, norm_mlp_post, w_in, w_out,
              attn_norm, ssm_conv_w, ssm_conv_b, ssm_a_log, ssm_dt_bias, ssm_d, ssm_norm,
              sconv_w, sconv_norm, w_mlp_up, w_mlp_down):
    weights = (norm_mix_pre, norm_mix_post, norm_mlp_pre, norm_mlp_post, w_in, w_out, attn_norm,
               ssm_conv_w, ssm_conv_b, ssm_a_log, ssm_dt_bias, ssm_d, ssm_norm, sconv_w,
               sconv_norm, w_mlp_up, w_mlp_down)
    y_prompt, (p_k, p_v, p_ssm, p_conv, p_sconv) = run_trunk(x_prompt, None, weights, True)
    states = (cache_attn_k, cache_attn_v, state_ssm, state_ssm_conv, state_sconv)
    y_sample, (s_k, s_v, s_ssm, s_conv, s_sconv) = run_trunk(x_sample, states, weights, False)
    return (y_prompt, y_sample, p_k, p_v, p_ssm, p_conv, p_sconv, s_k, s_v, s_ssm, s_conv, s_sconv)
```

```python
import contextlib
import os
DBG = set(os.environ.get('DBG_SKIP', '').split(','))
import numpy as np
import concourse.bass as bass
import concourse.mybir as mybir
from concourse.bass_utils import run_bass_kernel_spmd

F32 = mybir.dt.float32
BF16 = mybir.dt.bfloat16
AF = mybir.ActivationFunctionType
ALU = mybir.AluOpType

ENGS = ("pe", "act", "dve", "pool", "sp")
EPOCH = 20000
NDMA = 24


class H:
    __slots__ = ("name", "w", "rs", "bank")

    def __init__(self, name, bank=False):
        self.name = name
        self.w = None
        self.rs = []
        self.bank = bank


class _Phase:
    def __init__(self, rec, stack):
        self.rec = rec
        self.stack = stack

    def sb(self, name, shape, dtype):
        return self.stack.enter_context(self.rec.nc.sbuf_tensor(self.rec.uname(name), list(shape), dtype))

    def ps(self, name, shape, dtype):
        return self.stack.enter_context(self.rec.nc.psum_tensor(self.rec.uname(name), list(shape), dtype))


class Rec:
    def __init__(self, nc):
        self.nc = nc
        self.ops = {e: [] for e in ENGS}
        self.sigcount = {e: 0 for e in ENGS}
        self.seen = {e: {} for e in ENGS}
        self.dma_uses = [0] * NDMA
        self.dma_rr = 0
        self.handles = []
        self.csem = {}
        self.dsem = []
        self._n = 0
        self.barrier_deps = None
        self.first_done = {e: True for e in ENGS}
        self.bgsem = []
        self.swsem = []
        self.sw_uses = [0] * self.NSW
        self.sw_rr = 0
        self.bg_uses = [0] * 8
        self.bg_rr = 0

    def dma_bg(self, eng, fn):
        i = self.bg_rr
        self.bg_rr = (self.bg_rr + 1) % 8
        self.bg_uses[i] += 1
        rec = {"fn": fn, "sig": False, "dma": None, "bg": i, "deps": set()}
        if not self.first_done[eng]:
            self.first_done[eng] = True
            for d in self.barrier_deps:
                rec["deps"].add(d)
        self.ops[eng].append(rec)

    NSW = 12

    def dma_sw(self, fn, reads=(), writes=()):
        i = self.sw_rr
        self.sw_rr = (self.sw_rr + 1) % self.NSW
        self.sw_uses[i] += 1
        val = 16 * self.sw_uses[i]
        me = ("s", i, val)
        rec = {"fn": fn, "sig": False, "dma": ("s", i, val)}
        self.ops["pool"].append(rec)
        deps = self._track(me, "pool", reads, writes)
        if val > 16:
            deps.add(("s", i, val - 16))
        rec["deps"] = deps
        return me

    def final_wait(self, eng):
        deps = set(("b", i, 16 * self.bg_uses[i]) for i in range(8) if self.bg_uses[i])
        rec = {"fn": (lambda e: e.nop()), "sig": False, "dma": None, "deps": deps}
        self.ops[eng].append(rec)

    def uname(self, n):
        self._n += 1
        return "%s_%d" % (n, self._n)

    def H(self, name="h", bank=False):
        h = H(name, bank)
        self.handles.append(h)
        return h

    def Hs(self, name, n):
        return [self.H("%s%d" % (name, i)) for i in range(n)]

    @contextlib.contextmanager
    def top(self):
        with contextlib.ExitStack() as st:
            self.topstack = st
            self.tp = _Phase(self, st)
            for e in ENGS:
                if e == "sp":
                    continue
                self.csem[e] = [st.enter_context(self.nc.semaphore(self.uname("c" + e))) for _ in range(10)]
            self.dsem = [st.enter_context(self.nc.semaphore(self.uname("d"))) for _ in range(NDMA)]
            self.bgsem = [st.enter_context(self.nc.semaphore(self.uname("b"))) for _ in range(8)]
            self.swsem = [st.enter_context(self.nc.semaphore(self.uname("s"))) for _ in range(self.NSW)]
            with self.nc.Block() as block:
                @block.sync
                def _(sync):
                    for e in self.csem:
                        for s in self.csem[e]:
                            sync.sem_clear(s)
                    for s in self.dsem:
                        sync.sem_clear(s)
                    for s in self.bgsem:
                        sync.sem_clear(s)
                    for s in self.swsem:
                        sync.sem_clear(s)
            yield self

    @contextlib.contextmanager
    def phase(self, name):
        with contextlib.ExitStack() as st:
            ph = _Phase(self, st)
            yield ph
            self.flush()

    def _add_dep(self, deps, prod, eng, bank=False):
        if prod is None:
            return
        if prod[0] == "c":
            if prod[1] == eng and (eng == "pe" or bank):
                return
            self.ops[prod[1]][prod[2]]["sig"] = True
        deps.add(prod)

    def _track(self, me, eng, reads, writes):
        deps = set()
        writes = list(writes) + [h for h in reads if h.bank]
        reads = [h for h in reads if not h.bank]
        for h in reads:
            self._add_dep(deps, h.w, eng)
        for h in writes:
            self._add_dep(deps, h.w, eng, h.bank)
            for r in h.rs:
                self._add_dep(deps, r, eng, h.bank)
        if not self.first_done[eng]:
            self.first_done[eng] = True
            for d in self.barrier_deps:
                deps.add(d)
        for h in reads:
            h.rs.append(me)
        for h in writes:
            h.w = me
            h.rs = []
        deps.discard(me)
        return deps

    def op(self, eng, fn, reads=(), writes=()):
        idx = len(self.ops[eng])
        me = ("c", eng, idx)
        rec = {"fn": fn, "sig": False, "dma": None}
        self.ops[eng].append(rec)
        rec["deps"] = self._track(me, eng, reads, writes)
        return me

    def dma(self, eng, fn, reads=(), writes=()):
        semi = self.dma_rr
        self.dma_rr = (self.dma_rr + 1) % NDMA
        self.dma_uses[semi] += 1
        val = 16 * self.dma_uses[semi]
        me = ("d", semi, val)
        rec = {"fn": fn, "sig": False, "dma": (semi, val)}
        self.ops[eng].append(rec)
        deps = self._track(me, eng, reads, writes)
        if val > 16:
            deps.add(("d", semi, val - 16))
        rec["deps"] = deps
        return me

    def flush(self):
        bdeps = []
        for e in ENGS:
            if e != "sp" and self.ops[e]:
                for k in range(len(self.ops[e]) - 1, -1, -1):
                    if self.ops[e][k]["dma"] is None and self.ops[e][k].get("bg") is None:
                        self.ops[e][k]["sig"] = True
                        break
        rank = {}
        cnt = dict(self.sigcount)
        for e in ENGS:
            for k, o in enumerate(self.ops[e]):
                if o["sig"] and o["dma"] is None:
                    c = cnt[e]
                    rank[(e, k)] = (c // EPOCH, c % EPOCH + 1)
                    cnt[e] = c + 1
        for e in ENGS:
            last = None
            for k, o in enumerate(self.ops[e]):
                if o["sig"] and o["dma"] is None:
                    last = rank[(e, k)]
            if last is not None:
                bdeps.append(("r", e, last[0], last[1]))
        for semi in range(NDMA):
            if self.dma_uses[semi]:
                bdeps.append(("d", semi, 16 * self.dma_uses[semi]))
        for i in range(self.NSW):
            if self.sw_uses[i]:
                bdeps.append(("s", i, 16 * self.sw_uses[i]))

        ops = self.ops
        rec = self

        def emit(ename, eng):
            seen = rec.seen[ename]
            for k, o in enumerate(ops[ename]):
                for d in sorted(o["deps"], key=str):
                    if d[0] == "c":
                        ep, v = rank[(d[1], d[2])]
                        key = ("c", d[1], ep)
                        sem = rec.csem[d[1]][ep]
                    elif d[0] == "r":
                        ep, v = d[2], d[3]
                        key = ("c", d[1], ep)
                        sem = rec.csem[d[1]][ep]
                    elif d[0] == "b":
                        key = ("b", d[1])
                        v = d[2]
                        sem = rec.bgsem[d[1]]
                    elif d[0] == "s":
                        key = ("s", d[1])
                        v = d[2]
                        sem = rec.swsem[d[1]]
                    else:
                        key = ("d", d[1])
                        v = d[2]
                        sem = rec.dsem[d[1]]
                    if seen.get(key, 0) < v:
                        eng.wait_ge(sem, v)
                        seen[key] = v
                ins = o["fn"](eng)
                if o.get("bg") is not None:
                    ins.then_inc(rec.bgsem[o["bg"]], 16)
                elif o["dma"] is not None and o["dma"][0] == "s":
                    ins.then_inc(rec.swsem[o["dma"][1]], 16)
                elif o["dma"] is not None:
                    ins.then_inc(rec.dsem[o["dma"][0]], 16)
                elif o["sig"]:
                    ep, v = rank[(ename, k)]
                    ins.then_inc(rec.csem[ename][ep], 1)

        with self.nc.Block() as block:
            if ops["sp"]:
                @block.sync
                def _(e):
                    emit("sp", e)
            if ops["pe"]:
                @block.tensor
                def _(e):
                    emit("pe", e)
            if ops["dve"]:
                @block.vector
                def _(e):
                    emit("dve", e)
            if ops["act"]:
                @block.scalar
                def _(e):
                    emit("act", e)
            if ops["pool"]:
                @block.gpsimd
                def _(e):
                    emit("pool", e)

        self.sigcount = cnt
        self.ops = {e: [] for e in ENGS}
        for h in self.handles:
            h.w = None
            h.rs = []
        self.handles = []
        self.barrier_deps = bdeps
        self.first_done = {e: False for e in ENGS}


DM = 1024
DA = 256
DB = 512
DC = 256
HA = 4
HB = 8
NST = 128
HD = 64
DFF = 4096
INC = 3080
C_Q, C_K, C_V, C_Z, C_XBC, C_DT, C_BG, C_CG, C_UC = 0, 256, 512, 768, 1280, 2304, 2312, 2568, 2824
EPS = 1e-6
WOFF = 128


class Cfg:
    def __init__(self, T=4096, D=4, NS=4, WB=2048, SL=8, NQ=256, parts=("bg", "mixp", "mixs", "mlp"), stop=99, serial=False, w_att=2):
        self.w_att = w_att
        self.stop = stop
        self.serial = serial
        self.T, self.D, self.NS, self.WB, self.SL, self.NQ = T, D, NS, WB, SL, NQ
        self.parts = parts
        self.NTS = NS * SL
        self.WBP = min(2048, T)
        self.X = WOFF + 2048 + NQ
        self.NKT = WB // 128 + 1


def w_table():
    slopes = 2.0 ** (-8.0 * np.arange(1, HA + 1, dtype=np.float64) / HA)
    d = np.arange(0, 2049, dtype=np.float64)
    cnt = np.zeros(2049)
    for (win, dil) in ((128, 1), (512, 4), (2048, 16)):
        cnt += ((d % dil) == 0) & ((d // dil) <= (win // dil))
    return cnt[None, :] * np.exp(-slopes[:, None] * d[None, :])


def host_consts(cfg):
    w = w_table()
    X = cfg.X
    p = np.arange(128)[:, None]
    x = np.arange(X)[None, :]
    d = x - p - WOFF
    valid = (d >= 0) & (d <= 2048)
    dc = np.clip(d, 0, 2048)
    wsk = np.stack([np.where(valid, w[h][dc], 0.0) for h in range(HA)], axis=1)
    j = np.arange(cfg.NKT)[None, :, None]
    s = np.arange(cfg.SL)[None, None, :]
    pp = np.arange(128)[:, None, None]
    d2 = cfg.WB + s - 128 * j - pp
    v2 = (d2 >= 0) & (d2 <= 2048)
    d2c = np.clip(d2, 0, 2048)
    wsm = np.stack([np.where(v2, w[h][d2c], 0.0) for h in range(HA)], axis=1)
    k = np.arange(128)
    U = (k[:, None] <= k[None, :]).astype(np.float32)
    Lst = (k[:, None] > k[None, :]).astype(np.float32)
    return {
        "c_wsk": wsk.astype(np.float32), "c_wsm": wsm.astype(np.float32),
        "c_identf": np.eye(128, dtype=np.float32), "c_U": U, "c_Lst": Lst,
    }


class Prog:
    def __init__(self, cfg):
        self.cfg = cfg
        self.nc = bass.Bass("TRN2", target_bir_lowering=False)
        self.R = Rec(self.nc)
        self.dram = {}

    def din(self, name, shape):
        self.dram[name] = self.nc.dram_tensor(name, list(shape), F32, kind="ExternalInput").ap()
        return self.dram[name]

    def dout(self, name, shape):
        self.dram[name] = self.nc.dram_tensor(name, list(shape), F32, kind="ExternalOutput").ap()
        return self.dram[name]

    def dscr(self, name, shape):
        self.dram[name] = self.nc.dram_tensor(name, list(shape), F32).ap()
        return self.dram[name]

    def mm(self, out, lhsT, rhs, r, w, start=True, stop=True):
        self.R.op("pe", lambda e: e.matmul(out, lhsT=lhsT, rhs=rhs, start=start, stop=stop), r, w)

    def tr(self, out, in_, ident, r, w):
        self.R.op("pe", lambda e: e.transpose(out, in_, ident), r, w)

    def act(self, out, in_, func, r, w, scale=None, bias=None, eng="act", accum=None):
        kw = {}
        if accum is not None:
            kw["accum_out"] = accum
        if scale is not None:
            kw["scale"] = scale
        if bias is not None:
            kw["bias"] = bias
        self.R.op(eng, lambda e: e.activation(out=out, in_=in_, func=func, **kw), r, w)

    def tt(self, eng, out, in0, in1, op, r, w):
        self.R.op(eng, lambda e: e.tensor_tensor(out=out, in0=in0, in1=in1, op=op), r, w)

    def ts(self, eng, out, in0, s1, op0, r, w, s2=None, op1=None):
        if op1 is None and eng == "pool" and op0 == ALU.mult:
            self.R.op(eng, lambda e: e.tensor_scalar(out=out, in0=in0, scalar1=s1, scalar2=0.0, op0=op0, op1=ALU.add), r, w)
        elif op1 is None:
            self.R.op(eng, lambda e: e.tensor_scalar(out=out, in0=in0, scalar1=s1, scalar2=None, op0=op0), r, w)
        else:
            self.R.op(eng, lambda e: e.tensor_scalar(out=out, in0=in0, scalar1=s1, scalar2=s2, op0=op0, op1=op1), r, w)

    def stt(self, out, in0, scalar, in1, op0, op1, r, w):
        self.R.op("dve", lambda e: e.scalar_tensor_tensor(out=out, in0=in0, scalar=scalar, in1=in1, op0=op0, op1=op1), r, w)

    def cp(self, eng, out, in_, r, w):
        if eng == "act":
            self.R.op("act", lambda e: e.activation(out=out, in_=in_, func=AF.Copy), r, w)
        else:
            self.R.op(eng, lambda e: e.tensor_copy(out=out, in_=in_), r, w)

    def recip(self, out, in_, r, w):
        self.R.op("dve", lambda e: e.reciprocal(out=out, in_=in_), r, w)

    def memset(self, eng, ap, val, w):
        self.R.op(eng, lambda e: e.memset(ap, val), (), w)

    def ld(self, out, in_, w, r=(), q="sp"):
        self.R.dma(q, lambda e: e.dma_start(out=out, in_=in_), r, w)

    def ldc(self, out, in_, w, r=()):
        self.R.dma_sw(lambda e: e.dma_start(out=out, in_=in_, max_dma_last_dim=4096), r, w)

    def st(self, out, in_, r, w=(), q="sp"):
        self.R.dma(q, lambda e: e.dma_start(out=out, in_=in_), r, w)

    def declare(self):
        c = self.cfg
        T, D, NS, WB, NTS, WBP = c.T, c.D, c.NS, c.WB, c.NTS, c.WBP
        self.din("xT", [DM, T]); self.din("xsT", [DM, NTS])
        self.din("ckT", [D, NS, DA, WB]); self.din("cv", [D, NS, WB, DA])
        self.din("h0T", [D, NS, NST, DB]); self.din("conv0", [D, NS, DM, 3]); self.din("sconv0", [D, NS, DC, 2])
        self.din("w_in", [D, DM, INC]); self.din("w_out", [D, DM, DM])
        self.din("w_up", [D, DM, DFF]); self.din("w_dn", [D, DFF, DM])
        for n in ("gpre", "gpost", "gmpre", "gmpost"):
            self.din(n, [128, D, 8])
        self.din("attn_g", [64, D, 4]); self.din("conv_w", [128, D, 8, 4]); self.din("conv_b", [128, D, 8])
        self.din("sconv_w", [128, D, 2, 3]); self.din("sconv_g", [128, D, 2])
        self.din("dtb", [128, D, 8]); self.din("alog", [128, D, 8]); self.din("dsk", [128, D, 8])
        self.din("ssm_g", [128, D, DB])
        self.din("c_wsk", [128, HA, c.X]); self.din("c_wsm", [128, HA, c.NKT, c.SL])
        self.din("c_identf", [128, 128]); self.din("c_U", [128, 128]); self.din("c_Lst", [128, 128])
        self.dout("yT", [DM, T]); self.dout("ysT", [DM, NTS])
        self.dout("pkT", [D, DA, WBP]); self.dout("pv", [D, WBP, DA])
        self.dout("phT", [D, NST, DB]); self.dout("pconv", [D, DM, 3]); self.dout("psconv", [D, DC, 2])
        self.dout("skT", [D, NS, DA, WB]); self.dout("sv", [D, NS, WB, DA])
        self.dout("shT", [D, NS, NST, DB]); self.dout("sconv", [D, NS, DM, 3]); self.dout("ssconv", [D, NS, DC, 2])
        self.dscr("xa", [DM, T]); self.dscr("xb", [DM, T])
        self.dscr("xsa", [DM, NTS]); self.dscr("xsb", [DM, NTS])

    def build(self):
        c = self.cfg
        R = self.R
        d = self.dram
        self.declare()
        with R.top():
            with R.phase("consts") as ph:
                self.G = G = type("G", (), {})()
                tp = R.tp
                G.identf = tp.sb("identf", [128, 128], F32)
                G.identb = tp.sb("identb", [128, 128], BF16)
                G.U = tp.sb("U", [128, 128], F32)
                G.Lst = tp.sb("Lst", [128, 128], F32)
                G.onesf = tp.sb("onesf", [128, 128], F32)
                G.onesb = tp.sb("onesb", [128, 128], BF16)
                G.eps = tp.sb("eps", [128, 1], F32)
                D = c.D
                G.gpre = tp.sb("gpre", [128, D, 8], F32); G.gpost = tp.sb("gpost", [128, D, 8], F32)
                G.gmpre = tp.sb("gmpre", [128, D, 8], F32); G.gmpost = tp.sb("gmpost", [128, D, 8], F32)
                G.attn_g = tp.sb("attn_g", [64, D, 4], F32)
                G.conv_w = tp.sb("conv_w", [128, D, 8, 4], F32); G.conv_b = tp.sb("conv_b", [128, D, 8], F32)
                G.sconv_w = tp.sb("sconv_w", [128, D, 2, 3], F32); G.sconv_g = tp.sb("sconv_g", [128, D, 2], F32)
                G.dtb = tp.sb("dtb", [128, D, 8], F32); G.aneg = tp.sb("aneg", [128, D, 8], F32)
                G.dsk = tp.sb("dsk", [128, D, 8], F32)
                G.nconv_b = tp.sb("nconv_b", [128, D, 8], F32)
                hs = R.H("consts")
                for nm, t, src in (("identf", G.identf, "c_identf"), ("U", G.U, "c_U"), ("Lst", G.Lst, "c_Lst"),
                                   ("gpre", G.gpre, "gpre"), ("gpost", G.gpost, "gpost"), ("gmpre", G.gmpre, "gmpre"),
                                   ("gmpost", G.gmpost, "gmpost"), ("attn_g", G.attn_g, "attn_g"),
                                   ("conv_w", G.conv_w, "conv_w"), ("conv_b", G.conv_b, "conv_b"),
                                   ("sconv_w", G.sconv_w, "sconv_w"), ("sconv_g", G.sconv_g, "sconv_g"),
                                   ("dtb", G.dtb, "dtb"), ("aneg", G.aneg, "alog"), ("dsk", G.dsk, "dsk")):
                    self.ld(t[:], d[src], [R.H(nm)])
                hb = R.H("identb")
                self.ldc(G.identb[:], d["c_identf"], [hb])
                self.memset("dve", G.onesf[:], 1.0, [R.H()])
                self.memset("dve", G.onesb[:], 1.0, [R.H()])
                self.memset("dve", G.eps[:], EPS, [R.H()])
                for l in range(c.D if "bg" in c.parts else 0):
                    for b in range(c.NS):
                        R.dma_bg("sp", (lambda l=l, b=b: lambda e: e.dma_start(
                            out=d["skT"][l, b, :, 0:c.WB - c.SL], in_=d["ckT"][l, b, :, c.SL:c.WB]))())
                        R.dma_bg("sp", (lambda l=l, b=b: lambda e: e.dma_start(
                            out=d["sv"][l, b, 0:c.WB - c.SL, :], in_=d["cv"][l, b, c.SL:c.WB, :]))())
            with R.phase("consts2") as ph:
                ha = R.H("aneg")
                self.act(G.aneg[:], G.aneg[:], AF.Exp, [], [ha])
                self.ts("dve", G.aneg[:], G.aneg[:], -1.0, ALU.mult, [ha], [ha])
                self.ts("dve", G.nconv_b[:], G.conv_b[:], -1.0, ALU.mult, [], [R.H()])
            for l in range(c.D):
                last = (l == c.D - 1)
                src, mid, dst = ("xT" if l == 0 else "xa"), "xb", ("yT" if last else "xa")
                ssrc, smid, sdst = ("xsT" if l == 0 else "xsa"), "xsb", ("ysT" if last else "xsa")
                if "mixp" in c.parts or "mixs" in c.parts:
                    self.mixer_layer(l, d[src], d[mid], d[ssrc], d[smid])
                if "mlp" in c.parts:
                    if not ("mixp" in c.parts or "mixs" in c.parts):
                        mid, smid = src, ssrc
                    self.mlp_layer(l, d[mid], d[dst], d[smid], d[sdst], last)
            with R.phase("final") as ph:
                R.final_wait("sp")
        return self.nc

    def rms_stats(self, ph_bufs, src3, N, nchunk, kparts, psF, hsrc, hpsF, nfeat):
        B = ph_bufs
        G = self.G
        R = self.R
        self.act(B.sq[0:kparts, 0:nchunk, 0:N], src3, AF.Square, hsrc, [B.h_sq])
        for cc in range(nchunk):
            self.mm(psF[0:kparts, 0:N], G.onesb[0:kparts, 0:kparts], B.sq[0:kparts, cc, 0:N], [B.h_sq], [hpsF],
                    start=(cc == 0), stop=(cc == nchunk - 1))
        self.act(B.rs[0:kparts, 0:N], psF[0:kparts, 0:N], AF.Ln, [hpsF], [B.h_rs], scale=1.0 / nfeat,
                 bias=G.eps[0:kparts, 0:1])
        self.act(B.rstd[0:kparts, 0:N], B.rs[0:kparts, 0:N], AF.Exp, [B.h_rs], [B.h_rstd], scale=-0.5)
        return B.rstd

    def mlp_layer(self, l, src, dst, ssrc, sdst, last):
        c = self.cfg
        R = self.R
        G = self.G
        d = self.dram
        NQ = c.NQ
        with R.phase("mlp%d" % l) as ph:
            B = type("B", (), {})()
            B.up = ph.sb("up", [128, 8, DFF], BF16)
            B.dn = ph.sb("dn", [128, 32, DM], BF16)
            B.xt = [ph.sb("xt%d" % i, [128, 8, NQ], F32) for i in range(3)]
            B.hTs = [ph.sb("hT%d" % i, [128, 8, NQ], BF16) for i in range(2)]
            B.aT = ph.sb("aT", [128, 32, NQ], BF16)
            B.rls = [ph.sb("rl%d" % i, [128, NQ], F32) for i in range(3)]
            B.f = ph.sb("f", [128, 8, NQ], F32)
            SA = type("SA", (), {})()
            SB_ = type("SB_", (), {})()
            for S_ in (SA, SB_):
                S_.sq = ph.sb("sq", [128, 8, NQ], BF16)
                S_.rs = ph.sb("rs", [128, NQ], F32)
                S_.rstd = ph.sb("rstd", [128, NQ], F32)
                S_.h_sq, S_.h_rs, S_.h_rstd = R.H(), R.H(), R.H()
            psU = [ph.ps("psU%d" % i, [128, 512], F32) for i in range(2)]
            psD = [ph.ps("psD%d" % i, [128, 512], F32) for i in range(2)]
            psF = ph.ps("psF", [128, 512], F32)
            h_up = [[R.H() for _ in range(2)] for _ in range(8)]
            h_dn = [R.H() for _ in range(8)]
            h_xt = [R.Hs("xt0", 8), R.Hs("xt1", 8), R.Hs("xt2", 8)]
            h_hT, h_aT, h_rl, h_f = [R.Hs("hTa", 8), R.Hs("hTb", 8)], R.Hs("aT", 32), R.Hs("rl", 3), R.Hs("f", 8)
            h_psU, h_psD, h_psF = [R.H("u0", True), R.H("u1", True)], [R.H("d0", True), R.H("d1", True)], R.H("f", True)
            upv = d["w_up"][l].rearrange("(c p) n -> p c n", p=128)
            dnv = d["w_dn"][l].rearrange("(c p) n -> p c n", p=128)
            for kc in range(8):
                for hf in range(2):
                    self.ldc(B.up[:, kc, hf * 2048:(hf + 1) * 2048], upv[:, kc, hf * 2048:(hf + 1) * 2048], [h_up[kc][hf]])
            tiles = [(src[:, i * NQ:(i + 1) * NQ], dst[:, i * NQ:(i + 1) * NQ], NQ) for i in range(c.T // NQ)]
            tiles.append((ssrc[:, 0:c.NTS], sdst[:, 0:c.NTS], c.NTS))
            nt = len(tiles)

            def load(i):
                s_, _, N = tiles[i]
                self.ld(B.xt[i % 3][:, :, 0:N], s_.rearrange("(c p) n -> p c n", p=128), h_xt[i % 3])

            def pre_norm(i):
                N = tiles[i][2]
                xt, hx = B.xt[i % 3], h_xt[i % 3]
                rstd = self.rms_stats(SA, xt[:, :, 0:N], N, 8, 128, psF, hx, h_psF, DM)
                for cc in range(8):
                    self.stt(B.hTs[i % 2][:, cc, 0:N], xt[:, cc, 0:N], G.gmpre[:, l, cc:cc + 1], rstd[:, 0:N],
                             ALU.mult, ALU.mult, [hx[cc], SA.h_rstd], [h_hT[i % 2][cc]])

            def post_norm(i):
                _, d_, N = tiles[i]
                xt, hx = B.xt[i % 3], h_xt[i % 3]
                rstd = self.rms_stats(SB_, B.f[:, :, 0:N], N, 8, 128, psF, h_f, h_psF, DM)
                for cc in range(8):
                    self.stt(B.f[:, cc, 0:N], B.f[:, cc, 0:N], G.gmpost[:, l, cc:cc + 1], rstd[:, 0:N], ALU.mult, ALU.mult,
                             [h_f[cc], SB_.h_rstd], [h_f[cc]])
                    self.tt("pool", xt[:, cc, 0:N], B.f[:, cc, 0:N], xt[:, cc, 0:N], ALU.add, [h_f[cc], hx[cc]], [hx[cc]])
                self.st(d_.rearrange("(c p) n -> p c n", p=128), xt[:, :, 0:N], hx)

            load(0)
            for g in range(8):
                self.ldc(B.dn[:, 4 * g:4 * g + 4, :], dnv[:, 4 * g:4 * g + 4, :], [h_dn[g]])
            pre_norm(0)
            for i, (s_, d_, N) in enumerate(tiles):
                if i + 1 < nt:
                    load(i + 1)
                hT, hhT = B.hTs[i % 2], h_hT[i % 2]
                for j in range(32):
                    pu, hpu = psU[j % 2], h_psU[j % 2]
                    for kc in range(8):
                        self.mm(pu[:, 0:N], B.up[:, kc, j * 128:(j + 1) * 128], hT[:, kc, 0:N],
                                [h_up[kc][j // 16], hhT[kc]], [hpu], start=(kc == 0), stop=(kc == 7))
                    rl, hrl = B.rls[j % 3], h_rl[j % 3]
                    self.act(rl[:, 0:N], pu[:, 0:N], AF.Relu, [hpu], [hrl])
                    self.tt("dve" if j % 4 != 3 else "pool", B.aT[:, j, 0:N], rl[:, 0:N], rl[:, 0:N], ALU.mult, [hrl], [h_aT[j]])
                    if j == 3 and i > 0:
                        post_norm(i - 1)
                for dc in range(8):
                    pd, hpd = psD[dc % 2], h_psD[dc % 2]
                    for j in range(32):
                        self.mm(pd[:, 0:N], B.dn[:, j, dc * 128:(dc + 1) * 128], B.aT[:, j, 0:N],
                                [h_dn[j // 4], h_aT[j]], [hpd], start=(j == 0), stop=(j == 31))
                    self.cp("act", B.f[:, dc, 0:N], pd[:, 0:N], [hpd], [h_f[dc]])
                    if dc == 1 and i + 1 < nt:
                        pre_norm(i + 1)
            post_norm(nt - 1)

    def mixer_layer(self, l, src, dst, ssrc, sdst):
        c = self.cfg
        R = self.R
        G = self.G
        d = self.dram
        NQ = c.NQ
        with contextlib.ExitStack() as wst:
            Wp = _Phase(R, wst)
            W = type("W", (), {})()
            W.win = Wp.sb("win", [128, 8, INC], BF16)
            W.woa = Wp.sb("woa", [64, 4, DM], BF16)
            W.wor = Wp.sb("wor", [128, 6, DM], BF16)
            W.ssm_g = Wp.sb("ssm_g", [128, DB], F32)
            W.wsk = Wp.sb("wsk", [128, HA, c.X], BF16)
            W.wsm = Wp.sb("wsm", [128, HA, c.NKT, c.SL], BF16)
            self.W = W

            def issue_weight_loads():
                W.h_win = [[R.H() for _ in range(2)] for _ in range(8)]
                W.h_woa, W.h_wor, W.h_ssm_g, W.h_wsk, W.h_wsm = R.H(), [R.H(), R.H(), R.H()], R.H(), R.H(), R.H()
                winv = d["w_in"][l].rearrange("(c p) n -> p c n", p=128)
                for kc in range(8 if "win" not in DBG else 0):
                    self.ldc(W.win[:, kc, 0:1280], winv[:, kc, 0:1280], [W.h_win[kc][0]])
                for kc in range(8 if "win" not in DBG else 0):
                    self.ldc(W.win[:, kc, 1280:INC], winv[:, kc, 1280:INC], [W.h_win[kc][1]])
                self.ld(W.ssm_g[:], d["ssm_g"][:, l, :], [W.h_ssm_g])
                for h in range(HA if "wsk" not in DBG else 0):
                    xh = c.X // 2
                    self.ldc(W.wsk[:, h, 0:xh], d["c_wsk"][:, h, 0:xh], [W.h_wsk])
                    self.ldc(W.wsk[:, h, xh:c.X], d["c_wsk"][:, h, xh:c.X], [W.h_wsk])
                if "wsm" not in DBG:
                    self.ldc(W.wsm[:], d["c_wsm"], [W.h_wsm])
                if "woa" not in DBG:
                    self.ldc(W.woa[:], d["w_out"][l, 0:DA, :].rearrange("(h e) n -> e h n", e=64), [W.h_woa])
                worv = d["w_out"][l, DA:DM, :].rearrange("(c p) n -> p c n", p=128)
                for k3 in range(3 if "wor" not in DBG else 0):
                    self.ldc(W.wor[:, 2 * k3:2 * k3 + 2, :], worv[:, 2 * k3:2 * k3 + 2, :], [W.h_wor[k3]])

            def fresh_weight_handles():
                W.h_win = [[R.H() for _ in range(2)] for _ in range(8)]
                W.h_woa, W.h_wor, W.h_ssm_g, W.h_wsk, W.h_wsm = R.H(), [R.H(), R.H(), R.H()], R.H(), R.H(), R.H()

            with R.phase("mixp%d" % l) as ph:
                if "wl" not in DBG:
                    issue_weight_loads()
                else:
                    fresh_weight_handles()
                P = self.alloc_mixer_bufs(ph, prompt=True)
                nt = c.T // NQ if "mixp" in c.parts else 0
                epi = None
                for i in range(nt):
                    nxt = src[:, (i + 1) * NQ:(i + 2) * NQ] if i + 1 < nt else None
                    epi = self.mixer_tile(l, P, True, i, src[:, i * NQ:(i + 1) * NQ], dst[:, i * NQ:(i + 1) * NQ], NQ,
                                          preloaded=(i > 0), nxt=nxt, prev_epi=epi)
                if epi is not None:
                    for _ in epi:
                        pass
                if "fin" not in DBG:
                    self.st(d["phT"][l], P.hst[:], [P.h_hst])
                    self.st(d["pconv"][l].rearrange("(c p) t -> p c t", p=128), P.xbc[:, :, NQ:NQ + 3], P.h_xbc)
                    self.st(d["psconv"][l].rearrange("(c p) t -> p c t", p=128), P.cub[:, :, NQ:NQ + 2], P.h_cub)
            with R.phase("mixs%d" % l) as ph:
                fresh_weight_handles()
                P = self.alloc_mixer_bufs(ph, prompt=False)
                if "mixs" in c.parts:
                    for _ in self.mixer_tile(l, P, False, 0, ssrc[:, 0:c.NTS], sdst[:, 0:c.NTS], c.NTS):
                        pass

    def alloc_mixer_bufs(self, ph, prompt):
        c = self.cfg
        R = self.R
        NQ = c.NQ if prompt else c.NTS
        nseg = 1 if prompt else c.NS
        P = type("P", (), {})()
        P.prompt = prompt
        P.NB = NQ
        f32, bf = F32, BF16
        if prompt:
            P.NSLOT = min(c.T // 128, 18)
            P.KT = ph.sb("KT", [128, 2, P.NSLOT * 128], bf)
            P.Vst = ph.sb("Vst", [128, P.NSLOT, HA, HD + 1], bf)
            P.h_KT = R.Hs("KT", P.NSLOT)
            P.h_V = R.Hs("V", P.NSLOT)
        else:
            P.NSLOT = c.NKT
            P.KT = ph.sb("kallT", [128, 2, c.NKT * 128], bf)
            P.Vst = ph.sb("Vall", [128, c.NKT, HA, HD + 1], bf)
            P.vraw = ph.sb("vraw", [128, c.NKT - 1, DA], f32)
            P.vnew = ph.sb("vnew", [c.SL, c.NS, HA, HD], bf)
            P.h_KT = [R.H()] * 1
            P.h_V = [R.H()] * 1
            P.h_vraw, P.h_vnew = R.H(), R.Hs("vnew", c.NS)
        P.xt = ph.sb("xt", [128, 8, NQ], f32); P.h_xt = R.Hs("xt", 8)
        P.sq = ph.sb("sq", [128, 8, NQ], bf); P.h_sq = R.H()
        P.rs = ph.sb("rs", [128, NQ], f32); P.h_rs = R.H()
        P.rstd = ph.sb("rstd", [128, NQ], f32); P.h_rstd = R.H()
        P.hT = ph.sb("hT", [128, 8, NQ], bf); P.h_hT = R.Hs("hT", 8)
        P.QT = ph.sb("QT", [128, 2, NQ], bf); P.h_QT = R.Hs("QT", 2)
        P.kst = ph.sb("kst", [128, 2, NQ], f32); P.h_kst = R.Hs("kst", 2)
        P.xbc = ph.sb("xbc", [128, 8, NQ + 3 * nseg], f32); P.h_xbc = R.Hs("xbc", 8)
        P.xcx = ph.sb("xcx", [128, 4, NQ], f32); P.h_xcx = R.Hs("xcx", 4)
        P.xcb = ph.sb("xcb", [128, 4, NQ], bf); P.h_xcb = R.Hs("xcb", 4)
        P.cvt = ph.sb("cvt", [128, NQ], f32); P.h_cvt = R.H()
        P.cvs = ph.sb("cvs", [128, max(NQ, DB)], f32); P.h_cvs = R.H()
        P.cub = ph.sb("cub", [128, 2, NQ + 2 * nseg], f32); P.h_cub = R.Hs("cub", 2)
        P.bT = ph.sb("bT", [128, 2, NQ], f32); P.h_bT = R.Hs("bT", 2)
        P.yc = ph.sb("yc", [128, 2, NQ], f32); P.h_yc = R.Hs("yc", 2)
        P.csb = P.yc; P.h_csb = P.h_yc
        P.xr = None
        P.ycn = ph.sb("ycn", [128, 2, NQ], bf); P.h_ycn = R.Hs("ycn", 2)
        P.pexp = [ph.sb("pexp%d" % i, [128, max(NQ, 136)], bf) for i in range(3)]; P.h_pexp = R.Hs("pexp", 3)
        P.pw = [ph.sb("pw%d" % i, [128, max(NQ, 136)], bf) for i in range(4)]; P.h_pw = R.Hs("pw", 4)
        P.osb = ph.sb("osb", [128, NQ], f32); P.h_osb = R.H()
        P.cvt2 = ph.sb("cvt2", [128, NQ], f32); P.h_cvt2 = R.H()
        P.h_cvs2 = R.H()
        P.oan = ph.sb("oan", [64, HA, NQ], bf); P.h_oan = R.Hs("oan", 4)
        P.arA = ph.sb("arA", [128, 2048], f32); P.h_arA = R.Hs("arA", 4)
        P.arB = ph.sb("arB", [128, 2048], f32); P.h_arB = R.Hs("arB", 2)
        P.oa = P.arA[0:64, 1024:1024 + HA * NQ].rearrange("p (h n) -> p h n", h=HA)
        P.t1 = P.arA[:, 0:512]; P.t2 = P.arA[:, 512:1024]
        P.xtoks = [ph.sb("xtok%d" % i, [128, DB], f32) for i in range(2)]; P.h_xtoks = R.Hs("xtok", 2)
        P.Rm = P.arB[:, 0:1024].rearrange("p (h l) -> p h l", h=8)
        P.Dm = P.arB[:, 1024:2048].rearrange("p (h l) -> p h l", h=8)
        P.mix = P.arB[:, 0:8 * NQ].rearrange("p (c n) -> p c n", c=8)
        P.vst32 = P.kst[:, 1, :] if NQ >= DA else ph.sb("vst32", [128, DA], f32)
        P.h_vst32 = P.h_kst[1] if NQ >= DA else R.H()
        nch = 2 if prompt else c.NS
        P.sz = [ph.sb("sz%d" % i, [128, DB], f32) for i in range(nch)]; P.h_sz = R.Hs("sz", nch)
        P.dt = ph.sb("dt", [128, 4, 8], f32); P.h_dt = R.Hs("dt", 4)
        P.dta = ph.sb("dta", [128, 8], f32); P.h_dta = R.H()
        P.ea = ph.sb("ea", [128, 16], f32); P.h_ea = R.H()
        P.btoks = [ph.sb("btok%d" % i, [128, 256], bf) for i in range(2)]; P.h_btoks = R.Hs("btok", 2)
        P.GM = ph.sb("GM", [128, 2, 128], f32); P.h_GM = R.H()
        P.MT = ph.sb("MT", [128, 8, 128], bf); P.h_MT = R.H()
        P.xdt = ph.sb("xdt", [128, 8, HD], bf); P.h_xdt = R.H()
        P.xw = ph.sb("xw", [128, 8, HD], bf); P.h_xw = R.H()
        P.ss = ph.sb("ss", [128, 4], f32); P.h_ss = R.H()
        P.ybt = ph.sb("ybt", [128, DB], bf); P.h_ybt = R.H()
        P.ybT = ph.sb("ybT", [128, 4, NQ], bf); P.h_ybT = R.Hs("ybT", 4)
        P.hst = ph.sb("hst", [128, DB], f32); P.h_hst = R.H()
        P.hstb = ph.sb("hstb", [128, DB], bf); P.h_hstb = R.H()
        P.htmp = P.t2; P.h_htmp = P.h_arA[1]
        P.t3 = P.cvs[:, 0:DB]; P.h_t3 = [P.h_cvs, P.h_cvs2]
        P.psA = ph.ps("psA", [128, 512], f32); P.psB = ph.ps("psB", [128, 512], f32)
        P.psC = ph.ps("psC", [128, 512], f32); P.psD = ph.ps("psD", [128, 512], f32)
        P.psE = ph.ps("psE", [128, 512], f32); P.psF = ph.ps("psF", [128, 512], f32)
        P.psG = ph.ps("psG", [128, 512], f32); P.psH = ph.ps("psH", [128, 1024], bf)
        for n in "ABCDEFGH":
            setattr(P, "h_ps" + n, R.H("ps" + n, bank=True))
        P.h_psF2 = P.h_psF
        if "init" in DBG:
            return P
        self.memset("pool", P.Vst[:, :, :, HD:HD + 1], 1.0, P.h_V)
        if prompt:
            self.memset("pool", P.xbc[:, :, 0:3], 0.0, P.h_xbc)
            self.memset("pool", P.cub[:, :, 0:2], 0.0, P.h_cub)
        if prompt:
            self.memset("pool", P.hst[:], 0.0, [P.h_hst])
            self.memset("pool", P.hstb[:], 0.0, [P.h_hstb])
        else:
            self.memset("pool", P.KT[:, :, c.WB:c.NKT * 128], 0.0, P.h_KT)
            self.memset("pool", P.Vst[:, c.NKT - 1, :, 0:HD], 0.0, P.h_V)
        return P

    def mixer_tile(self, l, P, prompt, i, src, dst, N, preloaded=False, nxt=None, prev_epi=None):
        c = self.cfg
        R = self.R
        G = self.G
        W = self.W
        d = self.dram
        NQ = c.NQ
        T = c.T
        win = W.win
        hwin = W.h_win
        if prompt:
            q0 = i * NQ
            chunks = [(0, 128), (128, 128)]
            nseg, Ls = 1, NQ
        else:
            q0 = 0
            chunks = [(b * c.SL, c.SL) for b in range(c.NS)]
            nseg, Ls = c.NS, c.SL
        xbc4 = P.xbc[:, :, 0:nseg * (Ls + 3)].rearrange("p c (s t) -> p c s t", s=nseg)
        cub4 = P.cub[:, :, 0:nseg * (Ls + 2)].rearrange("p c (s t) -> p c s t", s=nseg)

        def seg3(ap2):
            return ap2.rearrange("p (s t) -> p s t", s=nseg)

        def load_norm(src_ap):
            self.ld(P.xt[:, :, 0:N], src_ap.rearrange("(c p) n -> p c n", p=128), P.h_xt)
            rstd_ = self.rms_stats(P, P.xt[:, :, 0:N], N, 8, 128, P.psF, P.h_xt, P.h_psF, DM)
            for cc in range(8):
                self.stt(P.hT[:, cc, 0:N], P.xt[:, cc, 0:N], G.gpre[:, l, cc:cc + 1], rstd_[:, 0:N], ALU.mult, ALU.mult,
                         [P.h_xt[cc], P.h_rstd], [P.h_hT[cc]])
        if not prompt:
            for b in range(c.NS):
                self.ld(xbc4[:, :, b, 0:3], d["conv0"][l, b].rearrange("(c p) t -> p c t", p=128), P.h_xbc)
                self.ld(cub4[:, :, b, 0:2], d["sconv0"][l, b].rearrange("(c p) t -> p c t", p=128), P.h_cub)
        if not preloaded:
            load_norm(src)


        pcnt = [0]
        steps = []

        def proj(col0, evac):
            steps.append(lambda: proj_now(col0, evac))

        PB6 = ((P.psA, P.h_psA), (P.psB, P.h_psB), (P.psC, P.h_psC), (P.psD, P.h_psD), (P.psE, P.h_psE), (P.psG, P.h_psG))

        def proj_now(col0, evac):
            ps, hps = PB6[pcnt[0] % 6]
            pcnt[0] += 1
            for kc in range(8):
                self.mm(ps[:, 0:N], win[:, kc, col0:col0 + 128], P.hT[:, kc, 0:N],
                        [hwin[kc][0 if col0 < 1280 else 1], P.h_hT[kc]], [hps], start=(kc == 0), stop=(kc == 7))
            evac(ps[:, 0:N], hps)

        def hk(j):
            return P.h_KT[j % len(P.h_KT)]

        def hv(j):
            return P.h_V[j % len(P.h_V)]

        need_kv_out = (not prompt) or (q0 >= T - c.WBP)
        for g in range(2 if "p_q" not in DBG else 0):
            proj(C_Q + 128 * g, lambda ps, hps, g=g: self.cp("act", P.QT[:, g, 0:N], ps, [hps], [P.h_QT[g]]))
        for g in range(2 if "p_k" not in DBG else 0):
            def ev_k(ps, hps, g=g):
                if prompt:
                    s0 = (2 * i) % P.NSLOT
                    self.cp("act", P.KT[:, g, s0 * 128:s0 * 128 + N], ps, [hps], [hk(2 * i), hk(2 * i + 1)])
                if need_kv_out:
                    self.cp("dve", P.kst[:, g, 0:N], ps, [hps], [P.h_kst[g]])
            proj(C_K + 128 * g, ev_k)
        def st_k():
            if prompt:
                o0 = q0 - (T - c.WBP)
                self.st(d["pkT"][l][:, o0:o0 + N].rearrange("(g p) n -> p g n", p=128), P.kst[:, :, 0:N], P.h_kst)
            else:
                for b in range(c.NS):
                    self.st(d["skT"][l, b][:, c.WB - c.SL:c.WB].rearrange("(g p) n -> p g n", p=128),
                            P.kst[:, :, b * c.SL:(b + 1) * c.SL], P.h_kst)
        if need_kv_out and "p_k" not in DBG:
            steps.append(st_k)
        for cc in range(8 if "p_xbc" not in DBG else 0):
            proj(C_XBC + 128 * cc, lambda ps, hps, cc=cc: self.cp(
                "dve", xbc4[:, cc, :, 3:3 + Ls], seg3(ps), [hps], [P.h_xbc[cc]]))
        for g in range(2 if "p_g" not in DBG else 0):
            proj(C_BG + 128 * g, lambda ps, hps, g=g: self.cp("act", P.bT[:, g, 0:N], ps, [hps], [P.h_bT[g]]))
        for g in range(2 if "p_g" not in DBG else 0):
            proj(C_CG + 128 * g, lambda ps, hps, g=g: self.cp("act", P.csb[:, g, 0:N], ps, [hps], [P.h_csb[g]]))
            proj(C_UC + 128 * g, lambda ps, hps, g=g: self.tt(
                "dve", cub4[:, g, :, 2:2 + Ls], seg3(P.csb[:, g, 0:N]), seg3(ps), ALU.mult,
                [hps, P.h_csb[g]], [P.h_cub[g]]))

        for st_ in steps:
            st_()
            if prev_epi is not None:
                try:
                    next(prev_epi)
                except StopIteration:
                    prev_epi = None
        if prev_epi is not None:
            for _ in prev_epi:
                pass

        def tm_v(ci, c0, L, psx, hpsx):
            for kc in range(8):
                self.mm(psx[0:L, 0:DA], P.hT[:, kc, c0:c0 + L], win[:, kc, C_V:C_V + DA],
                        [hwin[kc][0], P.h_hT[kc]], [hpsx], start=(kc == 0), stop=(kc == 7))
            psv4 = psx[0:L, 0:DA].rearrange("p (h e) -> p h e", h=HA)
            if prompt:
                j = 2 * i + ci
                self.cp("act", P.Vst[0:L, j % P.NSLOT, :, 0:HD], psv4, [hpsx], [hv(j)])
            else:
                self.cp("act", P.vnew[0:L, ci, :, :], psv4, [hpsx], [P.h_vnew[ci]])
            if need_kv_out:
                self.cp("dve", P.vst32[0:L, :], psx[0:L, 0:DA], [hpsx], [P.h_vst32])
                if prompt:
                    o0 = q0 + c0 - (T - c.WBP)
                    self.st(d["pv"][l, o0:o0 + L, :], P.vst32[0:L, :], [P.h_vst32])
                else:
                    self.st(d["sv"][l, ci, c.WB - c.SL:c.WB, :], P.vst32[0:L, :], [P.h_vst32])

        def tm_z(ci, c0, L):
            for kc in range(8):
                self.mm(P.psB[0:L, 0:DB], P.hT[:, kc, c0:c0 + L], win[:, kc, C_Z:C_Z + DB],
                        [hwin[kc][0], P.h_hT[kc]], [P.h_psB], start=(kc == 0), stop=(kc == 7))
            szb, hszb = P.sz[ci % len(P.sz)], P.h_sz[ci % len(P.sz)]
            self.act(szb[0:L, :], P.psB[0:L, 0:DB], AF.Exp, [P.h_psB], [hszb], scale=-1.0)
            self.act(szb[0:L, :], szb[0:L, :], AF.Ln, [hszb], [hszb], bias=1.0)
            self.act(szb[0:L, :], szb[0:L, :], AF.Exp, [hszb], [hszb], scale=-1.0)
            self.tt("dve", szb[0:L, :], szb[0:L, :], P.psB[0:L, 0:DB], ALU.mult, [hszb, P.h_psB], [hszb])

        def tm_dt(ci, c0, L):
            for kc in range(8):
                self.mm(P.psF[0:L, 0:8], P.hT[:, kc, c0:c0 + L], win[:, kc, C_DT:C_DT + 8],
                        [hwin[kc][1], P.h_hT[kc]], [P.h_psF], start=(kc == 0), stop=(kc == 7))
            self.tt("dve", P.dt[0:L, ci, :], P.psF[0:L, 0:8], G.dtb[0:L, l, :], ALU.add, [P.h_psF], [P.h_dt[ci]])
            self.act(P.dt[0:L, ci, :], P.dt[0:L, ci, :], AF.Exp, [P.h_dt[ci]], [P.h_dt[ci]])
            self.act(P.dt[0:L, ci, :], P.dt[0:L, ci, :], AF.Ln, [P.h_dt[ci]], [P.h_dt[ci]], bias=1.0)

        if not prompt:
            for ci, (c0, L) in enumerate(chunks):
                tm_v(ci, c0, L, P.psA, P.h_psA)
                tm_z(ci, c0, L)
                tm_dt(ci, c0, L)

        h_oa = [P.h_arA[2 + (h * P.NB) // 512] for h in range(HA)]
        NQB = P.NB

        def g_attention():
            if prompt:
                for ci, (c0, L) in enumerate(chunks):
                    tm_v(ci, c0, L, P.psC, P.h_psC)
                    yield
                jmin = max(0, (q0 - 2048) // 128)
                jmax = 2 * i + 1
                js = list(range(jmin, jmax + 1))
                nj = len(js)
                for h in range(HA):
                    half, g = (h % 2) * 64, h // 2

                    SB = ((P.psC, P.h_psC), (P.psD, P.h_psD), (P.psG, P.h_psG))

                    def S(jj):
                        j = js[jj]
                        ps, hps = SB[jj % 3]
                        s0 = (j % P.NSLOT) * 128
                        self.mm(ps[:, 0:N], P.KT[half:half + 64, g, s0:s0 + 128], P.QT[half:half + 64, g, 0:N],
                                [hk(j), P.h_QT[g]], [hps])

                    def E(jj):
                        ps, hps = SB[jj % 3]
                        self.act(P.pexp[jj % 3][:, 0:N], ps[:, 0:N], AF.Exp, [hps], [P.h_pexp[jj % 3]], scale=0.125)

                    def M(jj):
                        j = js[jj]
                        delta = q0 - 128 * j
                        self.tt("pool" if jj % 3 == 2 else "dve", P.pw[jj % 4][:, 0:N], P.pexp[jj % 3][:, 0:N],
                                W.wsk[:, h, WOFF + delta:WOFF + delta + N], ALU.mult,
                                [P.h_pexp[jj % 3], W.h_wsk], [P.h_pw[jj % 4]])

                    def V(jj):
                        j = js[jj]
                        self.mm(P.psE[0:HD + 1, 0:N], P.Vst[:, j % P.NSLOT, h, 0:HD + 1], P.pw[jj % 4][:, 0:N],
                                [hv(j), P.h_pw[jj % 4]], [P.h_psE], start=(jj == 0), stop=(jj == nj - 1))
                    S(0)
                    for it in range(nj + 4):
                        if it + 1 < nj:
                            S(it + 1)
                        if it < nj:
                            E(it)
                        if 0 <= it - 1 < nj:
                            M(it - 1)
                        if 0 <= it - 4 < nj:
                            V(it - 4)
                        yield
                    self.attn_norm_head(P, h, N, P.psE[0:HD + 1, 0:N], h_oa)
                    yield
            else:
                for b in range(c.NS):
                    for g in range(2):
                        self.ldc(P.KT[:, g, 0:c.WB], d["ckT"][l, b][g * 128:(g + 1) * 128, :], [P.h_KT[0]])
                    self.ld(P.vraw[:], d["cv"][l, b].rearrange("(j p) f -> p j f", p=128), [P.h_vraw])
                    for g in range(2):
                        self.cp("act", P.KT[:, g, c.WB:c.WB + c.SL], P.kst[:, g, b * c.SL:(b + 1) * c.SL],
                                [P.h_kst[g]], [P.h_KT[0]])
                    self.cp("pool", P.Vst[:, 0:c.NKT - 1, :, 0:HD], P.vraw[:].rearrange("p j (h e) -> p j h e", h=HA),
                            [P.h_vraw], [P.h_V[0]])
                    self.cp("pool", P.Vst[0:c.SL, c.NKT - 1, :, 0:HD], P.vnew[0:c.SL, b, :, :], [P.h_vnew[b]], [P.h_V[0]])
                    yield
                    for h in range(HA):
                        half, g = (h % 2) * 64, h // 2
                        ps, hps = ((P.psC, P.h_psC), (P.psD, P.h_psD))[h % 2]
                        nk = c.NKT
                        for j in range(nk):
                            self.mm(ps[:, j * c.SL:(j + 1) * c.SL], P.KT[half:half + 64, g, j * 128:(j + 1) * 128],
                                    P.QT[half:half + 64, g, b * c.SL:(b + 1) * c.SL], [P.h_KT[0], P.h_QT[g]], [hps])
                        self.act(P.pexp[h % 2][:, 0:nk * c.SL], ps[:, 0:nk * c.SL], AF.Exp, [hps], [P.h_pexp[h % 2]], scale=0.125)
                        self.tt("dve", P.pw[h % 2][:, 0:nk * c.SL], P.pexp[h % 2][:, 0:nk * c.SL],
                                W.wsm[:, h, :, :].rearrange("p j s -> p (j s)"), ALU.mult,
                                [P.h_pexp[h % 2], W.h_wsm], [P.h_pw[h % 2]])
                        o0 = h * c.NTS + b * c.SL
                        for j in range(nk):
                            self.mm(P.psE[0:HD + 1, o0:o0 + c.SL], P.Vst[:, j, h, 0:HD + 1],
                                    P.pw[h % 2][:, j * c.SL:(j + 1) * c.SL],
                                    [P.h_V[0], P.h_pw[h % 2]], [P.h_psE], start=(j == 0), stop=(j == nk - 1))
                        yield
                for h in range(HA):
                    self.attn_norm_head(P, h, N, P.psE[0:HD + 1, h * c.NTS:h * c.NTS + N], h_oa)
                    yield
            rstd = self.rms_stats(P, P.oa[0:64, :, 0:N], N, HA, 64, P.psF, list(set(h_oa)), P.h_psF, DA)
            for h in range(HA):
                self.stt(P.oan[0:64, h, 0:N], P.oa[0:64, h, 0:N], G.attn_g[0:64, l, h:h + 1], rstd[0:64, 0:N],
                         ALU.mult, ALU.mult, [h_oa[h], P.h_rstd], [P.h_oan[h]])
            yield

        def g_zdt():
            for ci, (c0, L) in enumerate(chunks):
                tm_z(ci, c0, L)
                tm_dt(ci, c0, L)
                yield

        def g_ssm():
            cvts = ((P.cvt, P.h_cvt, P.cvs[:, 0:NQB], P.h_cvs), (P.cvt2, P.h_cvt2, P.cvs[:, NQB:2 * NQB], P.h_cvs2))

            def conv_a(cc):
                cvt, hcvt, cvs, hcvs = cvts[cc % 2]
                c3 = seg3(cvt[:, 0:N])
                self.ts("dve", c3, xbc4[:, cc, :, 0:Ls], G.conv_w[:, l, cc, 0:1], ALU.mult, [P.h_xbc[cc]], [hcvt])
                for k in range(1, 4):
                    self.stt(c3, xbc4[:, cc, :, k:k + Ls], G.conv_w[:, l, cc, k:k + 1], c3, ALU.mult, ALU.add,
                             [P.h_xbc[cc], hcvt], [hcvt])
                self.act(cvs[:, 0:N], cvt[:, 0:N], AF.Exp, [hcvt], [hcvs], scale=-1.0, bias=G.nconv_b[:, l, cc:cc + 1])
                self.act(cvs[:, 0:N], cvs[:, 0:N], AF.Ln, [hcvs], [hcvs], bias=1.0)
                self.act(cvs[:, 0:N], cvs[:, 0:N], AF.Exp, [hcvs], [hcvs], scale=-1.0)

            def conv_b(cc):
                cvt, hcvt, cvs, hcvs = cvts[cc % 2]
                if cc < 4:
                    self.stt(P.xcx[:, cc, 0:N], cvt[:, 0:N], G.conv_b[:, l, cc:cc + 1], cvs[:, 0:N], ALU.add, ALU.mult,
                             [hcvt, hcvs], [P.h_xcx[cc]])
                else:
                    self.stt(P.xcb[:, cc - 4, 0:N], cvt[:, 0:N], G.conv_b[:, l, cc:cc + 1], cvs[:, 0:N], ALU.add, ALU.mult,
                             [hcvt, hcvs], [P.h_xcb[cc - 4]])
            early = prompt and not c.serial
            if early:
                g0 = self.ssd_chunk(l, P, prompt, 0, chunks[0][0], chunks[0][1])
                g1 = self.ssd_chunk(l, P, prompt, 1, chunks[1][0], chunks[1][1])
            conv_a(0)
            yield
            for cc in range(1, 8):
                conv_a(cc)
                conv_b(cc - 1)
                if early and cc == 6:
                    next(g0)
                yield
            conv_b(7)
            if prompt:
                self.cp("pool", P.xbc[:, :, 0:3], P.xbc[:, :, N:N + 3], P.h_xbc, P.h_xbc)
            else:
                for b in range(c.NS):
                    self.st(d["sconv"][l, b].rearrange("(c p) t -> p c t", p=128), xbc4[:, :, b, Ls:Ls + 3], P.h_xbc)
            if early:
                next(g0)
            yield
            if early:
                for _ in range(3):
                    next(g0)
                    yield
                live0 = True
                while True:
                    if live0:
                        try:
                            next(g0)
                        except StopIteration:
                            live0 = False
                    try:
                        next(g1)
                    except StopIteration:
                        break
                    yield
            else:
                for ci, (c0, L) in enumerate(chunks):
                    yield from self.ssd_chunk(l, P, prompt, ci, c0, L)

        def g_sconv():
            for g in range(2):
                self.ts("pool", P.yc[:, g, 0:N].rearrange("p (s t) -> p s t", s=nseg), cub4[:, g, :, 0:Ls],
                        G.sconv_w[:, l, g, 0:1], ALU.mult, [P.h_cub[g]], [P.h_yc[g]])
                yield
                yc3 = P.yc[:, g, 0:N].rearrange("p (s t) -> p s t", s=nseg)
                for k in range(1, 3):
                    self.stt(yc3, cub4[:, g, :, k:k + Ls], G.sconv_w[:, l, g, k:k + 1], yc3, ALU.mult, ALU.add,
                             [P.h_cub[g], P.h_yc[g]], [P.h_yc[g]])
                    yield
                self.tt("pool", P.yc[:, g, 0:N], P.yc[:, g, 0:N], P.bT[:, g, 0:N], ALU.mult, [P.h_yc[g], P.h_bT[g]], [P.h_yc[g]])
                yield
            if prompt:
                self.cp("pool", P.cub[:, :, 0:2], P.cub[:, :, N:N + 2], P.h_cub, P.h_cub)
            else:
                for b in range(c.NS):
                    self.st(d["ssconv"][l, b].rearrange("(c p) t -> p c t", p=128), cub4[:, :, b, Ls:Ls + 2], P.h_cub)
            rstd = self.rms_stats(P, P.yc[:, :, 0:N], N, 2, 128, P.psF, P.h_yc, P.h_psF, DC)
            for g in range(2):
                self.stt(P.ycn[:, g, 0:N], P.yc[:, g, 0:N], G.sconv_g[:, l, g:g + 1], rstd[:, 0:N], ALU.mult, ALU.mult,
                         [P.h_yc[g], P.h_rstd], [P.h_ycn[g]])
            yield

        def g_prefetch():
            for _ in range(10):
                yield
            load_norm(nxt)
            yield

        if c.serial:
            for gen in ((g_zdt(),) if prompt else ()) + (g_attention(), g_ssm(), g_sconv()) + ((g_prefetch(),) if nxt is not None else ()):
                for _ in gen:
                    pass
        else:
            gens = [[g_attention(), c.w_att], [g_ssm(), 1], [g_sconv(), 1]] + ([[g_zdt(), 1]] if prompt else []) \
                + ([[g_prefetch(), 1]] if nxt is not None else [])
            while gens:
                for ent in list(gens):
                    try:
                        for _ in range(ent[1]):
                            next(ent[0])
                    except StopIteration:
                        gens.remove(ent)

        h_mix = P.h_arB
        for dc in range(8):
            ps, hps = PB6[dc % 6]
            cs = slice(dc * 128, (dc + 1) * 128)
            for h in range(HA):
                self.mm(ps[:, 0:N], W.woa[0:64, h, cs], P.oan[0:64, h, 0:N], [W.h_woa, P.h_oan[h]], [hps],
                        start=(h == 0), stop=False)
            for cc in range(4):
                self.mm(ps[:, 0:N], W.wor[:, cc, cs], P.ybT[:, cc, 0:N], [W.h_wor[cc // 2], P.h_ybT[cc]], [hps],
                        start=False, stop=False)
            for g in range(2):
                self.mm(ps[:, 0:N], W.wor[:, 4 + g, cs], P.ycn[:, g, 0:N], [W.h_wor[2], P.h_ycn[g]], [hps],
                        start=False, stop=(g == 1))
            self.cp("act", P.mix[:, dc, 0:N], ps[:, 0:N], [hps], h_mix)
        def g_epi():
            self.ld(P.xt[:, :, 0:N], src.rearrange("(c p) n -> p c n", p=128), P.h_xt)
            rstd = self.rms_stats(P, P.mix[:, :, 0:N], N, 8, 128, P.psF, h_mix, P.h_psF, DM)
            yield
            for cc in range(8):
                self.stt(P.mix[:, cc, 0:N], P.mix[:, cc, 0:N], G.gpost[:, l, cc:cc + 1], rstd[:, 0:N], ALU.mult, ALU.mult,
                         h_mix + [P.h_rstd], h_mix)
                self.tt("pool", P.xt[:, cc, 0:N], P.mix[:, cc, 0:N], P.xt[:, cc, 0:N], ALU.add,
                        h_mix + [P.h_xt[cc]], [P.h_xt[cc]])
                yield
            self.st(dst.rearrange("(c p) n -> p c n", p=128), P.xt[:, :, 0:N], P.h_xt)
            yield
        return g_epi()

    def attn_norm_head(self, P, h, N, pso, h_oa):
        G = self.G
        self.cp("act", P.osb[0:HD + 1, 0:N], pso, [P.h_psE], [P.h_osb])
        self.act(P.osb[64:65, 0:N], P.osb[64:65, 0:N], AF.Ln, [P.h_osb], [P.h_osb])
        self.act(P.osb[64:65, 0:N], P.osb[64:65, 0:N], AF.Exp, [P.h_osb], [P.h_osb], scale=-1.0)
        self.mm(P.psF[0:64, 0:N], G.onesf[64:65, 0:64], P.osb[64:65, 0:N], [P.h_osb], [P.h_psF])
        self.tt("dve", P.oa[0:64, h, 0:N], P.osb[0:64, 0:N], P.psF[0:64, 0:N], ALU.mult, [P.h_osb, P.h_psF], [h_oa[h]])

    def ssd_chunk(self, l, P, prompt, ci, c0, L):
        c = self.cfg
        R = self.R
        G = self.G
        W = self.W
        d = self.dram
        hA = P.h_arA
        h_t1, h_t2, h_t3 = hA[0], hA[1], P.h_t3
        xtok, h_xtok = P.xtoks[ci % 2], P.h_xtoks[ci % 2]
        btok, h_btok = P.btoks[ci % 2], P.h_btoks[ci % 2]
        h_Rm, h_Dm = P.h_arB[0], P.h_arB[1]
        L4 = 4 * L
        if not prompt:
            self.ld(P.hst[:], d["h0T"][l, ci], [P.h_hst])
            self.cp("pool", P.hstb[:], P.hst[:], [P.h_hst], [P.h_hstb])
        dt = P.dt[0:L, ci, :]
        xt3 = xtok[0:L, :].rearrange("p (h e) -> p h e", h=HB)
        for cc in range(4):
            self.tr(P.psF[0:L, cc * 128:(cc + 1) * 128], P.xcx[:, cc, c0:c0 + L], G.identf[:, :], [P.h_xcx[cc]], [P.h_psF])
        self.cp("act", xtok[0:L, :], P.psF[0:L, 0:DB], [P.h_psF], [h_xtok])
        for g in range(2):
            self.tr(P.psH[0:L, g * 128:(g + 1) * 128], P.xcb[:, g, c0:c0 + L], G.identb[:, :], [P.h_xcb[g]], [P.h_psH])
        self.cp("dve", btok[0:L, :], P.psH[0:L, 0:256], [P.h_psH], [h_btok])
        self.tt("dve", P.dta[0:L, :], dt, G.aneg[0:L, l, :], ALU.mult, [P.h_dt[ci]], [P.h_dta])
        self.tt("dve", P.Rm[0:L, :, 0:L], G.U[0:L, 0:L].unsqueeze(1).broadcast_to([L, 8, L]),
                P.dta[0:L, :].unsqueeze(2).broadcast_to([L, 8, L]), ALU.mult, [P.h_dta], [h_Rm])
        yield
        for hh in range(2):
            self.mm(P.psB[0:L, 0:L4].rearrange("p (h l) -> p h l", h=4), G.Lst[0:L, 0:L], P.Rm[0:L, 4 * hh:4 * hh + 4, 0:L],
                    [h_Rm], [P.h_psB])
            self.act(P.Dm[0:L, 4 * hh:4 * hh + 4, 0:L], P.psB[0:L, 0:L4].rearrange("p (h l) -> p h l", h=4), AF.Exp,
                     [P.h_psB], [h_Dm])
            if hh == 0:
                for g in range(2):
                    self.mm(P.psF[0:L, g * 128:g * 128 + L], P.xcb[:, g, c0:c0 + L], P.xcb[:, 2 + g, c0:c0 + L],
                            [P.h_xcb[g], P.h_xcb[2 + g]], [P.h_psF])
                self.tt("dve", P.GM[0:L, :, 0:L], P.psF[0:L, 0:256].rearrange("p (g l) -> p g l", g=2)[:, :, 0:L],
                        G.U[0:L, 0:L].unsqueeze(1).broadcast_to([L, 2, L]), ALU.mult, [P.h_psF], [P.h_GM])
                self.tt("pool", P.xdt[0:L, :, :], xt3, dt.unsqueeze(2).broadcast_to([L, HB, HD]), ALU.mult,
                        [h_xtok, P.h_dt[ci]], [P.h_xdt])
            else:
                self.mm(P.psF[0:L, 400:408], G.U[0:L, 0:L], P.dta[0:L, :], [P.h_dta], [P.h_psF])
                self.mm(P.psF[0:128, 408:416], G.onesf[0:L, 0:128], P.dta[0:L, :], [P.h_dta], [P.h_psF])
                self.act(P.ea[0:L, 0:8], P.psF[0:L, 400:408], AF.Exp, [P.h_psF], [P.h_ea])
                self.act(P.ea[:, 8:16], P.psF[:, 408:416], AF.Exp, [P.h_psF], [P.h_ea])
            yield
        for g in range(2):
            self.tt("dve" if g == 0 else "pool", P.MT[0:L, 4 * g:4 * g + 4, 0:L], P.Dm[0:L, 4 * g:4 * g + 4, 0:L],
                    P.GM[0:L, g:g + 1, 0:L].broadcast_to([L, 4, L]), ALU.mult, [h_Dm, P.h_GM], [P.h_MT])
        self.tt("dve", P.xw[0:L, :, :], P.xdt[0:L, :, :], P.Dm[0:L, :, L - 1:L].broadcast_to([L, HB, HD]), ALU.mult,
                [P.h_xdt, h_Dm], [P.h_xw])
        yield
        for h in range(HB):
            self.mm(P.psA[0:L, h * HD:(h + 1) * HD], P.MT[0:L, h, 0:L], P.xdt[0:L, h, :], [P.h_MT, P.h_xdt], [P.h_psA])
        for g in range(2):
            self.mm(P.psB[0:L, g * 256:(g + 1) * 256], P.xcb[:, 2 + g, c0:c0 + L], P.hstb[:, g * 256:(g + 1) * 256],
                    [P.h_xcb[2 + g], P.h_hstb], [P.h_psB])
        yield
        for g in range(2):
            self.mm(P.psF[:, g * 256:(g + 1) * 256], btok[0:L, g * 128:(g + 1) * 128],
                    P.xw[0:L, 4 * g:4 * g + 4, :].rearrange("p h e -> p (h e)"), [h_btok, P.h_xw], [P.h_psF])
        self.tt("pool", P.htmp[:].rearrange("p (h e) -> p h e", h=HB), P.hst[:].rearrange("p (h e) -> p h e", h=HB),
                P.ea[:, 8:16].unsqueeze(2).broadcast_to([128, HB, HD]), ALU.mult, [P.h_hst, P.h_ea], [P.h_htmp])
        self.tt("dve", P.hst[:], P.htmp[:], P.psF[:, 0:DB], ALU.add, [P.h_htmp, P.h_psF], [P.h_hst])
        self.cp("pool", P.hstb[:], P.hst[:], [P.h_hst], [P.h_hstb])
        if not prompt:
            self.st(d["shT"][l, ci], P.hst[:], [P.h_hst])
        t1_3 = P.t1[0:L, :].rearrange("p (h e) -> p h e", h=HB)
        self.tt("dve", t1_3, P.psB[0:L, 0:DB].rearrange("p (h e) -> p h e", h=HB),
                P.ea[0:L, 0:8].unsqueeze(2).broadcast_to([L, HB, HD]), ALU.mult, [P.h_psB, P.h_ea], [h_t1])
        self.tt("dve", P.t1[0:L, :], P.t1[0:L, :], P.psA[0:L, 0:DB], ALU.add, [h_t1, P.h_psA], [h_t1])
        yield
        szi = P.sz[ci % len(P.sz)]
        hszi = P.h_sz[ci % len(P.sz)]
        t3_3 = P.t3[0:L, :].rearrange("p (h e) -> p h e", h=HB)
        self.tt("dve", t3_3, xt3, G.dsk[0:L, l, :].unsqueeze(2).broadcast_to([L, HB, HD]), ALU.mult, [h_xtok], h_t3)
        self.tt("dve", P.t1[0:L, :], P.t1[0:L, :], P.t3[0:L, :], ALU.add, [h_t1] + h_t3, [h_t1])
        self.tt("dve", P.t1[0:L, :], P.t1[0:L, :], szi[0:L, :], ALU.mult, [h_t1, hszi], [h_t1])
        yield
        self.act(P.t3[0:L, :], P.t1[0:L, :], AF.Square, [h_t1], h_t3 + [P.h_ss], accum=P.ss[0:L, 0:1])
        self.act(P.ss[0:L, 1:2], P.ss[0:L, 0:1], AF.Ln, [P.h_ss], [P.h_ss], scale=1.0 / DB, bias=G.eps[0:L, 0:1])
        self.act(P.ss[0:L, 2:3], P.ss[0:L, 1:2], AF.Exp, [P.h_ss], [P.h_ss], scale=-0.5)
        self.stt(P.ybt[0:L, :], P.t1[0:L, :], P.ss[0:L, 2:3], W.ssm_g[0:L, :], ALU.mult, ALU.mult,
                 [h_t1, P.h_ss, W.h_ssm_g], [P.h_ybt])
        yield
        for cc in range(4):
            self.tr(P.psH[:, 256 + cc * 128:256 + cc * 128 + L], P.ybt[0:L, cc * 128:(cc + 1) * 128], G.identb[0:L, 0:L],
                    [P.h_ybt], [P.h_psH])
        self.cp("act", P.ybT[:, :, c0:c0 + L], P.psH[:, 256:768].rearrange("p (c l) -> p c l", c=4)[:, :, 0:L],
                [P.h_psH], P.h_ybT)
        yield


def _pvec(v, D, nchunk):
    return np.ascontiguousarray(np.asarray(v, np.float32).reshape(D, nchunk, 128).transpose(2, 0, 1))


def _bvec(v, D):
    v = np.asarray(v, np.float32)
    return np.ascontiguousarray(np.broadcast_to(v[None], (128,) + v.shape))


def make_in_maps(inp, cfg, ncores, nseq):
    D, NS, WB, T = cfg.D, cfg.NS, cfg.WB, cfg.T
    f = lambda a: np.ascontiguousarray(np.asarray(a, np.float32))
    shared = {
        "w_in": f(inp["w_in"]), "w_out": f(inp["w_out"]), "w_up": f(inp["w_mlp_up"]), "w_dn": f(inp["w_mlp_down"]),
        "gpre": _pvec(inp["norm_mix_pre"], D, 8), "gpost": _pvec(inp["norm_mix_post"], D, 8),
        "gmpre": _pvec(inp["norm_mlp_pre"], D, 8), "gmpost": _pvec(inp["norm_mlp_post"], D, 8),
        "attn_g": f(np.asarray(inp["attn_norm"], np.float32).reshape(D, 4, 64).transpose(2, 0, 1)),
        "conv_w": f(np.asarray(inp["ssm_conv_w"], np.float32).reshape(D, 4, 8, 128).transpose(3, 0, 2, 1)),
        "conv_b": _pvec(inp["ssm_conv_b"], D, 8),
        "sconv_w": f(np.asarray(inp["sconv_w"], np.float32).reshape(D, 3, 2, 128).transpose(3, 0, 2, 1)),
        "sconv_g": _pvec(inp["sconv_norm"], D, 2),
        "dtb": _bvec(inp["ssm_dt_bias"], D), "alog": _bvec(inp["ssm_a_log"], D), "dsk": _bvec(inp["ssm_d"], D),
        "ssm_g": _bvec(inp["ssm_norm"], D),
    }
    shared.update(host_consts(cfg))
    xp = np.asarray(inp["x_prompt"], np.float32)
    xs = np.asarray(inp["x_sample"], np.float32)
    ck = np.asarray(inp["cache_attn_k"], np.float32)
    cv = np.asarray(inp["cache_attn_v"], np.float32)
    hs = np.asarray(inp["state_ssm"], np.float32)
    cs = np.asarray(inp["state_ssm_conv"], np.float32)
    ss = np.asarray(inp["state_sconv"], np.float32)
    maps = []
    for r in range(ncores):
        seq = r % nseq
        sl = slice(NS * r, NS * r + NS)
        m = dict(shared)
        m["xT"] = f(xp[seq].T)
        m["xsT"] = f(xs[sl].reshape(NS * cfg.SL, DM).T)
        m["ckT"] = f(ck[:, sl].reshape(D, NS, WB, DA).transpose(0, 1, 3, 2))
        m["cv"] = f(cv[:, sl].reshape(D, NS, WB, DA))
        m["h0T"] = f(hs[:, sl].reshape(D, NS, DB, NST).transpose(0, 1, 3, 2))
        m["conv0"] = f(cs[:, sl].transpose(0, 1, 3, 2))
        m["sconv0"] = f(ss[:, sl].transpose(0, 1, 3, 2))
        maps.append(m)
    return maps


def assemble(results, cfg, ncores, nseq, nsample):
    D, NS, WB, T, WBP = cfg.D, cfg.NS, cfg.WB, cfg.T, cfg.WBP
    yp = np.zeros((nseq, T, DM), np.float32)
    ys = np.zeros((nsample, cfg.SL, DM), np.float32)
    pk = np.zeros((D, nseq, WBP, HA, HD), np.float32)
    pv = np.zeros_like(pk)
    pssm = np.zeros((D, nseq, HB, HD, NST), np.float32)
    pconv = np.zeros((D, nseq, 3, DM), np.float32)
    psconv = np.zeros((D, nseq, 2, DC), np.float32)
    sk = np.zeros((D, nsample, WB, HA, HD), np.float32)
    sv = np.zeros_like(sk)
    sssm = np.zeros((D, nsample, HB, HD, NST), np.float32)
    sconv = np.zeros((D, nsample, 3, DM), np.float32)
    ssconv = np.zeros((D, nsample, 2, DC), np.float32)
    for r in range(ncores):
        o = results[r]
        sl = slice(NS * r, NS * r + NS)
        if r < nseq:
            yp[r] = o["yT"].T
            pk[:, r] = o["pkT"].transpose(0, 2, 1).reshape(D, WBP, HA, HD)
            pv[:, r] = o["pv"].reshape(D, WBP, HA, HD)
            pssm[:, r] = o["phT"].transpose(0, 2, 1).reshape(D, HB, HD, NST)
            pconv[:, r] = o["pconv"].transpose(0, 2, 1)
            psconv[:, r] = o["psconv"].transpose(0, 2, 1)
        ys[sl] = o["ysT"].T.reshape(NS, cfg.SL, DM)
        sk[:, sl] = o["skT"].transpose(0, 1, 3, 2).reshape(D, NS, WB, HA, HD)
        sv[:, sl] = o["sv"].reshape(D, NS, WB, HA, HD)
        sssm[:, sl] = o["shT"].transpose(0, 1, 3, 2).reshape(D, NS, HB, HD, NST)
        sconv[:, sl] = o["sconv"].transpose(0, 1, 3, 2)
        ssconv[:, sl] = o["ssconv"].transpose(0, 1, 3, 2)
    return (yp, ys, pk, pv, pssm, pconv, psconv, sk, sv, sssm, sconv, ssconv)


def run(inp, cfg, ncores, nseq):
    prog = Prog(cfg)
    nc = prog.build()
    maps = make_in_maps(inp, cfg, ncores, nseq)
    res = run_bass_kernel_spmd(nc, maps, core_ids=list(range(ncores)))
    return assemble(res.results, cfg, ncores, nseq, ncores * cfg.NS)


def kernel(**inputs):
    cfg = Cfg(T=4096, D=4, NS=4)
    return run(inputs, cfg, 8, 4)
```

```python
import contextlib
import os
DBG = set(os.environ.get('DBG_SKIP', '').split(','))
import numpy as np
import concourse.bass as bass
import concourse.mybir as mybir
from concourse.bass_utils import run_bass_kernel_spmd

F32 = mybir.dt.float32
BF16 = mybir.dt.bfloat16
AF = mybir.ActivationFunctionType
ALU = mybir.AluOpType

ENGS = ("pe", "act", "dve", "pool", "sp")
EPOCH = 20000
NDMA = 24


class H:
    __slots__ = ("name", "w", "rs", "bank")

    def __init__(self, name, bank=False):
        self.name = name
        self.w = None
        self.rs = []
        self.bank = bank


class _Phase:
    def __init__(self, rec, stack):
        self.rec = rec
        self.stack = stack

    def sb(self, name, shape, dtype):
        return self.stack.enter_context(self.rec.nc.sbuf_tensor(self.rec.uname(name), list(shape), dtype))

    def ps(self, name, shape, dtype):
        return self.stack.enter_context(self.rec.nc.psum_tensor(self.rec.uname(name), list(shape), dtype))


class Rec:
    def __init__(self, nc):
        self.nc = nc
        self.ops = {e: [] for e in ENGS}
        self.sigcount = {e: 0 for e in ENGS}
        self.seen = {e: {} for e in ENGS}
        self.dma_uses = [0] * NDMA
        self.dma_rr = 0
        self.handles = []
        self.csem = {}
        self.dsem = []
        self._n = 0
        self.barrier_deps = None
        self.first_done = {e: True for e in ENGS}
        self.bgsem = []
        self.swsem = []
        self.sw_uses = [0] * self.NSW
        self.sw_rr = 0
        self.bg_uses = [0] * 8
        self.bg_rr = 0

    def dma_bg(self, eng, fn):
        i = self.bg_rr
        self.bg_rr = (self.bg_rr + 1) % 8
        self.bg_uses[i] += 1
        rec = {"fn": fn, "sig": False, "dma": None, "bg": i, "deps": set()}
        if not self.first_done[eng]:
            self.first_done[eng] = True
            for d in self.barrier_deps:
                rec["deps"].add(d)
        self.ops[eng].append(rec)

    NSW = 12

    def dma_sw(self, fn, reads=(), writes=()):
        i = self.sw_rr
        self.sw_rr = (self.sw_rr + 1) % self.NSW
        self.sw_uses[i] += 1
        val = 16 * self.sw_uses[i]
        me = ("s", i, val)
        rec = {"fn": fn, "sig": False, "dma": ("s", i, val)}
        self.ops["pool"].append(rec)
        deps = self._track(me, "pool", reads, writes)
        if val > 16:
            deps.add(("s", i, val - 16))
        rec["deps"] = deps
        return me

    def final_wait(self, eng):
        deps = set(("b", i, 16 * self.bg_uses[i]) for i in range(8) if self.bg_uses[i])
        rec = {"fn": (lambda e: e.nop()), "sig": False, "dma": None, "deps": deps}
        self.ops[eng].append(rec)

    def uname(self, n):
        self._n += 1
        return "%s_%d" % (n, self._n)

    def H(self, name="h", bank=False):
        h = H(name, bank)
        self.handles.append(h)
        return h

    def Hs(self, name, n):
        return [self.H("%s%d" % (name, i)) for i in range(n)]

    @contextlib.contextmanager
    def top(self):
        with contextlib.ExitStack() as st:
            self.topstack = st
            self.tp = _Phase(self, st)
            for e in ENGS:
                if e == "sp":
                    continue
                self.csem[e] = [st.enter_context(self.nc.semaphore(self.uname("c" + e))) for _ in range(10)]
            self.dsem = [st.enter_context(self.nc.semaphore(self.uname("d"))) for _ in range(NDMA)]
            self.bgsem = [st.enter_context(self.nc.semaphore(self.uname("b"))) for _ in range(8)]
            self.swsem = [st.enter_context(self.nc.semaphore(self.uname("s"))) for _ in range(self.NSW)]
            with self.nc.Block() as block:
                @block.sync
                def _(sync):
                    for e in self.csem:
                        for s in self.csem[e]:
                            sync.sem_clear(s)
                    for s in self.dsem:
                        sync.sem_clear(s)
                    for s in self.bgsem:
                        sync.sem_clear(s)
                    for s in self.swsem:
                        sync.sem_clear(s)
            yield self

    @contextlib.contextmanager
    def phase(self, name):
        with contextlib.ExitStack() as st:
            ph = _Phase(self, st)
            yield ph
            self.flush()

    def _add_dep(self, deps, prod, eng, bank=False):
        if prod is None:
            return
        if prod[0] == "c":
            if prod[1] == eng and (eng == "pe" or bank):
                return
            self.ops[prod[1]][prod[2]]["sig"] = True
        deps.add(prod)

    def _track(self, me, eng, reads, writes):
        deps = set()
        writes = list(writes) + [h for h in reads if h.bank]
        reads = [h for h in reads if not h.bank]
        for h in reads:
            self._add_dep(deps, h.w, eng)
        for h in writes:
            self._add_dep(deps, h.w, eng, h.bank)
            for r in h.rs:
                self._add_dep(deps, r, eng, h.bank)
        if not self.first_done[eng]:
            self.first_done[eng] = True
            for d in self.barrier_deps:
                deps.add(d)
        for h in reads:
            h.rs.append(me)
        for h in writes:
            h.w = me
            h.rs = []
        deps.discard(me)
        return deps

    def op(self, eng, fn, reads=(), writes=()):
        idx = len(self.ops[eng])
        me = ("c", eng, idx)
        rec = {"fn": fn, "sig": False, "dma": None}
        self.ops[eng].append(rec)
        rec["deps"] = self._track(me, eng, reads, writes)
        return me

    def dma(self, eng, fn, reads=(), writes=()):
        semi = self.dma_rr
        self.dma_rr = (self.dma_rr + 1) % NDMA
        self.dma_uses[semi] += 1
        val = 16 * self.dma_uses[semi]
        me = ("d", semi, val)
        rec = {"fn": fn, "sig": False, "dma": (semi, val)}
        self.ops[eng].append(rec)
        deps = self._track(me, eng, reads, writes)
        if val > 16:
            deps.add(("d", semi, val - 16))
        rec["deps"] = deps
        return me

    def flush(self):
        bdeps = []
        for e in ENGS:
            if e != "sp" and self.ops[e]:
                for k in range(len(self.ops[e]) - 1, -1, -1):
                    if self.ops[e][k]["dma"] is None and self.ops[e][k].get("bg") is None:
                        self.ops[e][k]["sig"] = True
                        break
        rank = {}
        cnt = dict(self.sigcount)
        for e in ENGS:
            for k, o in enumerate(self.ops[e]):
                if o["sig"] and o["dma"] is None:
                    c = cnt[e]
                    rank[(e, k)] = (c // EPOCH, c % EPOCH + 1)
                    cnt[e] = c + 1
        for e in ENGS:
            last = None
            for k, o in enumerate(self.ops[e]):
                if o["sig"] and o["dma"] is None:
                    last = rank[(e, k)]
            if last is not None:
                bdeps.append(("r", e, last[0], last[1]))
        for semi in range(NDMA):
            if self.dma_uses[semi]:
                bdeps.append(("d", semi, 16 * self.dma_uses[semi]))
        for i in range(self.NSW):
            if self.sw_uses[i]:
                bdeps.append(("s", i, 16 * self.sw_uses[i]))

        ops = self.ops
        rec = self

        def emit(ename, eng):
            seen = rec.seen[ename]
            for k, o in enumerate(ops[ename]):
                for d in sorted(o["deps"], key=str):
                    if d[0] == "c":
                        ep, v = rank[(d[1], d[2])]
                        key = ("c", d[1], ep)
                        sem = rec.csem[d[1]][ep]
                    elif d[0] == "r":
                        ep, v = d[2], d[3]
                        key = ("c", d[1], ep)
                        sem = rec.csem[d[1]][ep]
                    elif d[0] == "b":
                        key = ("b", d[1])
                        v = d[2]
                        sem = rec.bgsem[d[1]]
                    elif d[0] == "s":
                        key = ("s", d[1])
                        v = d[2]
                        sem = rec.swsem[d[1]]
                    else:
                        key = ("d", d[1])
                        v = d[2]
                        sem = rec.dsem[d[1]]
                    if seen.get(key, 0) < v:
                        eng.wait_ge(sem, v)
                        seen[key] = v
                ins = o["fn"](eng)
                if o.get("bg") is not None:
                    ins.then_inc(rec.bgsem[o["bg"]], 16)
                elif o["dma"] is not None and o["dma"][0] == "s":
                    ins.then_inc(rec.swsem[o["dma"][1]], 16)
                elif o["dma"] is not None:
                    ins.then_inc(rec.dsem[o["dma"][0]], 16)
                elif o["sig"]:
                    ep, v = rank[(ename, k)]
                    ins.then_inc(rec.csem[ename][ep], 1)

        with self.nc.Block() as block:
            if ops["sp"]:
                @block.sync
                def _(e):
                    emit("sp", e)
            if ops["pe"]:
                @block.tensor
                def _(e):
                    emit("pe", e)
            if ops["dve"]:
                @block.vector
                def _(e):
                    emit("dve", e)
            if ops["act"]:
                @block.scalar
                def _(e):
                    emit("act", e)
            if ops["pool"]:
                @block.gpsimd
                def _(e):
                    emit("pool", e)

        self.sigcount = cnt
        self.ops = {e: [] for e in ENGS}
        for h in self.handles:
            h.w = None
            h.rs = []
        self.handles = []
        self.barrier_deps = bdeps
        self.first_done = {e: False for e in ENGS}


DM = 1024
DA = 256
DB = 512
DC = 256
HA = 4
HB = 8
NST = 128
HD = 64
DFF = 4096
INC = 3080
C_Q, C_K, C_V, C_Z, C_XBC, C_DT, C_BG, C_CG, C_UC = 0, 256, 512, 768, 1280, 2304, 2312, 2568, 2824
EPS = 1e-6
WOFF = 128


class Cfg:
    def __init__(self, T=4096, D=4, NS=4, WB=2048, SL=8, NQ=256, parts=("bg", "mixp", "mixs", "mlp"), stop=99, serial=False, w_att=2):
        self.w_att = w_att
        self.stop = stop
        self.serial = serial
        self.T, self.D, self.NS, self.WB, self.SL, self.NQ = T, D, NS, WB, SL, NQ
        self.parts = parts
        self.NTS = NS * SL
        self.WBP = min(2048, T)
        self.X = WOFF + 2048 + NQ
        self.NKT = WB // 128 + 1


def w_table():
    slopes = 2.0 ** (-8.0 * np.arange(1, HA + 1, dtype=np.float64) / HA)
    d = np.arange(0, 2049, dtype=np.float64)
    cnt = np.zeros(2049)
    for (win, dil) in ((128, 1), (512, 4), (2048, 16)):
        cnt += ((d % dil) == 0) & ((d // dil) <= (win // dil))
    return cnt[None, :] * np.exp(-slopes[:, None] * d[None, :])


def host_consts(cfg):
    w = w_table()
    X = cfg.X
    p = np.arange(128)[:, None]
    x = np.arange(X)[None, :]
    d = x - p - WOFF
    valid = (d >= 0) & (d <= 2048)
    dc = np.clip(d, 0, 2048)
    wsk = np.stack([np.where(valid, w[h][dc], 0.0) for h in range(HA)], axis=1)
    j = np.arange(cfg.NKT)[None, :, None]
    s = np.arange(cfg.SL)[None, None, :]
    pp = np.arange(128)[:, None, None]
    d2 = cfg.WB + s - 128 * j - pp
    v2 = (d2 >= 0) & (d2 <= 2048)
    d2c = np.clip(d2, 0, 2048)
    wsm = np.stack([np.where(v2, w[h][d2c], 0.0) for h in range(HA)], axis=1)
    k = np.arange(128)
    U = (k[:, None] <= k[None, :]).astype(np.float32)
    Lst = (k[:, None] > k[None, :]).astype(np.float32)
    return {
        "c_wsk": wsk.astype(np.float32), "c_wsm": wsm.astype(np.float32),
        "c_identf": np.eye(128, dtype=np.float32), "c_U": U, "c_Lst": Lst,
    }


class Prog:
    def __init__(self, cfg):
        self.cfg = cfg
        self.nc = bass.Bass("TRN2", target_bir_lowering=False)
        self.R = Rec(self.nc)
        self.dram = {}

    def din(self, name, shape):
        self.dram[name] = self.nc.dram_tensor(name, list(shape), F32, kind="ExternalInput").ap()
        return self.dram[name]

    def dout(self, name, shape):
        self.dram[name] = self.nc.dram_tensor(name, list(shape), F32, kind="ExternalOutput").ap()
        return self.dram[name]

    def dscr(self, name, shape):
        self.dram[name] = self.nc.dram_tensor(name, list(shape), F32).ap()
        return self.dram[name]

    def mm(self, out, lhsT, rhs, r, w, start=True, stop=True):
        self.R.op("pe", lambda e: e.matmul(out, lhsT=lhsT, rhs=rhs, start=start, stop=stop), r, w)

    def tr(self, out, in_, ident, r, w):
        self.R.op("pe", lambda e: e.transpose(out, in_, ident), r, w)

    def act(self, out, in_, func, r, w, scale=None, bias=None, eng="act", accum=None):
        kw = {}
        if accum is not None:
            kw["accum_out"] = accum
        if scale is not None:
            kw["scale"] = scale
        if bias is not None:
            kw["bias"] = bias
        self.R.op(eng, lambda e: e.activation(out=out, in_=in_, func=func, **kw), r, w)

    def tt(self, eng, out, in0, in1, op, r, w):
        self.R.op(eng, lambda e: e.tensor_tensor(out=out, in0=in0, in1=in1, op=op), r, w)

    def ts(self, eng, out, in0, s1, op0, r, w, s2=None, op1=None):
        if op1 is None and eng == "pool" and op0 == ALU.mult:
            self.R.op(eng, lambda e: e.tensor_scalar(out=out, in0=in0, scalar1=s1, scalar2=0.0, op0=op0, op1=ALU.add), r, w)
        elif op1 is None:
            self.R.op(eng, lambda e: e.tensor_scalar(out=out, in0=in0, scalar1=s1, scalar2=None, op0=op0), r, w)
        else:
            self.R.op(eng, lambda e: e.tensor_scalar(out=out, in0=in0, scalar1=s1, scalar2=s2, op0=op0, op1=op1), r, w)

    def stt(self, out, in0, scalar, in1, op0, op1, r, w):
        self.R.op("dve", lambda e: e.scalar_tensor_tensor(out=out, in0=in0, scalar=scalar, in1=in1, op0=op0, op1=op1), r, w)

    def cp(self, eng, out, in_, r, w):
        if eng == "act":
            self.R.op("act", lambda e: e.activation(out=out, in_=in_, func=AF.Copy), r, w)
        else:
            self.R.op(eng, lambda e: e.tensor_copy(out=out, in_=in_), r, w)

    def recip(self, out, in_, r, w):
        self.R.op("dve", lambda e: e.reciprocal(out=out, in_=in_), r, w)

    def memset(self, eng, ap, val, w):
        self.R.op(eng, lambda e: e.memset(ap, val), (), w)

    def ld(self, out, in_, w, r=(), q="sp"):
        self.R.dma(q, lambda e: e.dma_start(out=out, in_=in_), r, w)

    def ldc(self, out, in_, w, r=()):
        self.R.dma_sw(lambda e: e.dma_start(out=out, in_=in_, max_dma_last_dim=4096), r, w)

    def st(self, out, in_, r, w=(), q="sp"):
        self.R.dma(q, lambda e: e.dma_start(out=out, in_=in_), r, w)

    def declare(self):
        c = self.cfg
        T, D, NS, WB, NTS, WBP = c.T, c.D, c.NS, c.WB, c.NTS, c.WBP
        self.din("xT", [DM, T]); self.din("xsT", [DM, NTS])
        self.din("ckT", [D, NS, DA, WB]); self.din("cv", [D, NS, WB, DA])
        self.din("h0T", [D, NS, NST, DB]); self.din("conv0", [D, NS, DM, 3]); self.din("sconv0", [D, NS, DC, 2])
        self.din("w_in", [D, DM, INC]); self.din("w_out", [D, DM, DM])
        self.din("w_up", [D, DM, DFF]); self.din("w_dn", [D, DFF, DM])
        for n in ("gpre", "gpost", "gmpre", "gmpost"):
            self.din(n, [128, D, 8])
        self.din("attn_g", [64, D, 4]); self.din("conv_w", [128, D, 8, 4]); self.din("conv_b", [128, D, 8])
        self.din("sconv_w", [128, D, 2, 3]); self.din("sconv_g", [128, D, 2])
        self.din("dtb", [128, D, 8]); self.din("alog", [128, D, 8]); self.din("dsk", [128, D, 8])
        self.din("ssm_g", [128, D, DB])
        self.din("c_wsk", [128, HA, c.X]); self.din("c_wsm", [128, HA, c.NKT, c.SL])
        self.din("c_identf", [128, 128]); self.din("c_U", [128, 128]); self.din("c_Lst", [128, 128])
        self.dout("yT", [DM, T]); self.dout("ysT", [DM, NTS])
        self.dout("pkT", [D, DA, WBP]); self.dout("pv", [D, WBP, DA])
        self.dout("phT", [D, NST, DB]); self.dout("pconv", [D, DM, 3]); self.dout("psconv", [D, DC, 2])
        self.dout("skT", [D, NS, DA, WB]); self.dout("sv", [D, NS, WB, DA])
        self.dout("shT", [D, NS, NST, DB]); self.dout("sconv", [D, NS, DM, 3]); self.dout("ssconv", [D, NS, DC, 2])
        self.dscr("xa", [DM, T]); self.dscr("xb", [DM, T])
        self.dscr("xsa", [DM, NTS]); self.dscr("xsb", [DM, NTS])

    def build(self):
        c = self.cfg
        R = self.R
        d = self.dram
        self.declare()
        with R.top():
            with R.phase("consts") as ph:
                self.G = G = type("G", (), {})()
                tp = R.tp
                G.identf = tp.sb("identf", [128, 128], F32)
                G.identb = tp.sb("identb", [128, 128], BF16)
                G.U = tp.sb("U", [128, 128], F32)
                G.Lst = tp.sb("Lst", [128, 128], F32)
                G.onesf = tp.sb("onesf", [128, 128], F32)
                G.onesb = tp.sb("onesb", [128, 128], BF16)
                G.eps = tp.sb("eps", [128, 1], F32)
                D = c.D
                G.gpre = tp.sb("gpre", [128, D, 8], F32); G.gpost = tp.sb("gpost", [128, D, 8], F32)
                G.gmpre = tp.sb("gmpre", [128, D, 8], F32); G.gmpost = tp.sb("gmpost", [128, D, 8], F32)
                G.attn_g = tp.sb("attn_g", [64, D, 4], F32)
                G.conv_w = tp.sb("conv_w", [128, D, 8, 4], F32); G.conv_b = tp.sb("conv_b", [128, D, 8], F32)
                G.sconv_w = tp.sb("sconv_w", [128, D, 2, 3], F32); G.sconv_g = tp.sb("sconv_g", [128, D, 2], F32)
                G.dtb = tp.sb("dtb", [128, D, 8], F32); G.aneg = tp.sb("aneg", [128, D, 8], F32)
                G.dsk = tp.sb("dsk", [128, D, 8], F32)
                G.nconv_b = tp.sb("nconv_b", [128, D, 8], F32)
                hs = R.H("consts")
                for nm, t, src in (("identf", G.identf, "c_identf"), ("U", G.U, "c_U"), ("Lst", G.Lst, "c_Lst"),
                                   ("gpre", G.gpre, "gpre"), ("gpost", G.gpost, "gpost"), ("gmpre", G.gmpre, "gmpre"),
                                   ("gmpost", G.gmpost, "gmpost"), ("attn_g", G.attn_g, "attn_g"),
                                   ("conv_w", G.conv_w, "conv_w"), ("conv_b", G.conv_b, "conv_b"),
                                   ("sconv_w", G.sconv_w, "sconv_w"), ("sconv_g", G.sconv_g, "sconv_g"),
                                   ("dtb", G.dtb, "dtb"), ("aneg", G.aneg, "alog"), ("dsk", G.dsk, "dsk")):
                    self.ld(t[:], d[src], [R.H(nm)])
                hb = R.H("identb")
                self.ldc(G.identb[:], d["c_identf"], [hb])
                self.memset("dve", G.onesf[:], 1.0, [R.H()])
                self.memset("dve", G.onesb[:], 1.0, [R.H()])
                self.memset("dve", G.eps[:], EPS, [R.H()])
                for l in range(c.D if "bg" in c.parts else 0):
                    for b in range(c.NS):
                        R.dma_bg("sp", (lambda l=l, b=b: lambda e: e.dma_start(
                            out=d["skT"][l, b, :, 0:c.WB - c.SL], in_=d["ckT"][l, b, :, c.SL:c.WB]))())
                        R.dma_bg("sp", (lambda l=l, b=b: lambda e: e.dma_start(
                            out=d["sv"][l, b, 0:c.WB - c.SL, :], in_=d["cv"][l, b, c.SL:c.WB, :]))())
            with R.phase("consts2") as ph:
                ha = R.H("aneg")
                self.act(G.aneg[:], G.aneg[:], AF.Exp, [], [ha])
                self.ts("dve", G.aneg[:], G.aneg[:], -1.0, ALU.mult, [ha], [ha])
                self.ts("dve", G.nconv_b[:], G.conv_b[:], -1.0, ALU.mult, [], [R.H()])
            for l in range(c.D):
                last = (l == c.D - 1)
                src, mid, dst = ("xT" if l == 0 else "xa"), "xb", ("yT" if last else "xa")
                ssrc, smid, sdst = ("xsT" if l == 0 else "xsa"), "xsb", ("ysT" if last else "xsa")
                if "mixp" in c.parts or "mixs" in c.parts:
                    self.mixer_layer(l, d[src], d[mid], d[ssrc], d[smid])
                if "mlp" in c.parts:
                    if not ("mixp" in c.parts or "mixs" in c.parts):
                        mid, smid = src, ssrc
                    self.mlp_layer(l, d[mid], d[dst], d[smid], d[sdst], last)
            with R.phase("final") as ph:
                R.final_wait("sp")
        return self.nc

    def rms_stats(self, ph_bufs, src3, N, nchunk, kparts, psF, hsrc, hpsF, nfeat):
        B = ph_bufs
        G = self.G
        R = self.R
        self.act(B.sq[0:kparts, 0:nchunk, 0:N], src3, AF.Square, hsrc, [B.h_sq])
        for cc in range(nchunk):
            self.mm(psF[0:kparts, 0:N], G.onesb[0:kparts, 0:kparts], B.sq[0:kparts, cc, 0:N], [B.h_sq], [hpsF],
                    start=(cc == 0), stop=(cc == nchunk - 1))
        self.act(B.rs[0:kparts, 0:N], psF[0:kparts, 0:N], AF.Ln, [hpsF], [B.h_rs], scale=1.0 / nfeat,
                 bias=G.eps[0:kparts, 0:1])
        self.act(B.rstd[0:kparts, 0:N], B.rs[0:kparts, 0:N], AF.Exp, [B.h_rs], [B.h_rstd], scale=-0.5)
        return B.rstd

    def mlp_layer(self, l, src, dst, ssrc, sdst, last):
        c = self.cfg
        R = self.R
        G = self.G
        d = self.dram
        NQ = c.NQ
        with R.phase("mlp%d" % l) as ph:
            B = type("B", (), {})()
            B.up = ph.sb("up", [128, 8, DFF], BF16)
            B.dn = ph.sb("dn", [128, 32, DM], BF16)
            B.xt = [ph.sb("xt%d" % i, [128, 8, NQ], F32) for i in range(3)]
            B.hTs = [ph.sb("hT%d" % i, [128, 8, NQ], BF16) for i in range(2)]
            B.aT = ph.sb("aT", [128, 32, NQ], BF16)
            B.rls = [ph.sb("rl%d" % i, [128, NQ], F32) for i in range(3)]
            B.f = ph.sb("f", [128, 8, NQ], F32)
            SA = type("SA", (), {})()
            SB_ = type("SB_", (), {})()
            for S_ in (SA, SB_):
                S_.sq = ph.sb("sq", [128, 8, NQ], BF16)
                S_.rs = ph.sb("rs", [128, NQ], F32)
                S_.rstd = ph.sb("rstd", [128, NQ], F32)
                S_.h_sq, S_.h_rs, S_.h_rstd = R.H(), R.H(), R.H()
            psU = [ph.ps("psU%d" % i, [128, 512], F32) for i in range(2)]
            psD = [ph.ps("psD%d" % i, [128, 512], F32) for i in range(2)]
            psF = ph.ps("psF", [128, 512], F32)
            h_up = [[R.H() for _ in range(2)] for _ in range(8)]
            h_dn = [R.H() for _ in range(8)]
            h_xt = [R.Hs("xt0", 8), R.Hs("xt1", 8), R.Hs("xt2", 8)]
            h_hT, h_aT, h_rl, h_f = [R.Hs("hTa", 8), R.Hs("hTb", 8)], R.Hs("aT", 32), R.Hs("rl", 3), R.Hs("f", 8)
            h_psU, h_psD, h_psF = [R.H("u0", True), R.H("u1", True)], [R.H("d0", True), R.H("d1", True)], R.H("f", True)
            upv = d["w_up"][l].rearrange("(c p) n -> p c n", p=128)
            dnv = d["w_dn"][l].rearrange("(c p) n -> p c n", p=128)
            for kc in range(8):
                for hf in range(2):
                    self.ldc(B.up[:, kc, hf * 2048:(hf + 1) * 2048], upv[:, kc, hf * 2048:(hf + 1) * 2048], [h_up[kc][hf]])
            tiles = [(src[:, i * NQ:(i + 1) * NQ], dst[:, i * NQ:(i + 1) * NQ], NQ) for i in range(c.T // NQ)]
            tiles.append((ssrc[:, 0:c.NTS], sdst[:, 0:c.NTS], c.NTS))
            nt = len(tiles)

            def load(i):
                s_, _, N = tiles[i]
                self.ld(B.xt[i % 3][:, :, 0:N], s_.rearrange("(c p) n -> p c n", p=128), h_xt[i % 3])

            def pre_norm(i):
                N = tiles[i][2]
                xt, hx = B.xt[i % 3], h_xt[i % 3]
                rstd = self.rms_stats(SA, xt[:, :, 0:N], N, 8, 128, psF, hx, h_psF, DM)
                for cc in range(8):
                    self.stt(B.hTs[i % 2][:, cc, 0:N], xt[:, cc, 0:N], G.gmpre[:, l, cc:cc + 1], rstd[:, 0:N],
                             ALU.mult, ALU.mult, [hx[cc], SA.h_rstd], [h_hT[i % 2][cc]])

            def post_norm(i):
                _, d_, N = tiles[i]
                xt, hx = B.xt[i % 3], h_xt[i % 3]
                rstd = self.rms_stats(SB_, B.f[:, :, 0:N], N, 8, 128, psF, h_f, h_psF, DM)
                for cc in range(8):
                    self.stt(B.f[:, cc, 0:N], B.f[:, cc, 0:N], G.gmpost[:, l, cc:cc + 1], rstd[:, 0:N], ALU.mult, ALU.mult,
                             [h_f[cc], SB_.h_rstd], [h_f[cc]])
                    self.tt("pool", xt[:, cc, 0:N], B.f[:, cc, 0:N], xt[:, cc, 0:N], ALU.add, [h_f[cc], hx[cc]], [hx[cc]])
                self.st(d_.rearrange("(c p) n -> p c n", p=128), xt[:, :, 0:N], hx)

            load(0)
            for g in range(8):
                self.ldc(B.dn[:, 4 * g:4 * g + 4, :], dnv[:, 4 * g:4 * g + 4, :], [h_dn[g]])
            pre_norm(0)
            for i, (s_, d_, N) in enumerate(tiles):
                if i + 1 < nt:
                    load(i + 1)
                hT, hhT = B.hTs[i % 2], h_hT[i % 2]
                for j in range(32):
                    pu, hpu = psU[j % 2], h_psU[j % 2]
                    for kc in range(8):
                        self.mm(pu[:, 0:N], B.up[:, kc, j * 128:(j + 1) * 128], hT[:, kc, 0:N],
                                [h_up[kc][j // 16], hhT[kc]], [hpu], start=(kc == 0), stop=(kc == 7))
                    rl, hrl = B.rls[j % 3], h_rl[j % 3]
                    self.act(rl[:, 0:N], pu[:, 0:N], AF.Relu, [hpu], [hrl])
                    self.tt("dve" if j % 4 != 3 else "pool", B.aT[:, j, 0:N], rl[:, 0:N], rl[:, 0:N], ALU.mult, [hrl], [h_aT[j]])
                    if j == 3 and i > 0:
                        post_norm(i - 1)
                for dc in range(8):
                    pd, hpd = psD[dc % 2], h_psD[dc % 2]
                    for j in range(32):
                        self.mm(pd[:, 0:N], B.dn[:, j, dc * 128:(dc + 1) * 128], B.aT[:, j, 0:N],
                                [h_dn[j // 4], h_aT[j]], [hpd], start=(j == 0), stop=(j == 31))
                    self.cp("act", B.f[:, dc, 0:N], pd[:, 0:N], [hpd], [h_f[dc]])
                    if dc == 1 and i + 1 < nt:
                        pre_norm(i + 1)
            post_norm(nt - 1)

    def mixer_layer(self, l, src, dst, ssrc, sdst):
        c = self.cfg
        R = self.R
        G = self.G
        d = self.dram
        NQ = c.NQ
        with contextlib.ExitStack() as wst:
            Wp = _Phase(R, wst)
            W = type("W", (), {})()
            W.win = Wp.sb("win", [128, 8, INC], BF16)
            W.woa = Wp.sb("woa", [64, 4, DM], BF16)
            W.wor = Wp.sb("wor", [128, 6, DM], BF16)
            W.ssm_g = Wp.sb("ssm_g", [128, DB], F32)
            W.wsk = Wp.sb("wsk", [128, HA, c.X], BF16)
            W.wsm = Wp.sb("wsm", [128, HA, c.NKT, c.SL], BF16)
            self.W = W

            def issue_weight_loads():
                W.h_win = [[R.H() for _ in range(2)] for _ in range(8)]
                W.h_woa, W.h_wor, W.h_ssm_g, W.h_wsk, W.h_wsm = R.H(), [R.H(), R.H(), R.H()], R.H(), R.H(), R.H()
                winv = d["w_in"][l].rearrange("(c p) n -> p c n", p=128)
                for kc in range(8 if "win" not in DBG else 0):
                    self.ldc(W.win[:, kc, 0:1280], winv[:, kc, 0:1280], [W.h_win[kc][0]])
                for kc in range(8 if "win" not in DBG else 0):
                    self.ldc(W.win[:, kc, 1280:INC], winv[:, kc, 1280:INC], [W.h_win[kc][1]])
                self.ld(W.ssm_g[:], d["ssm_g"][:, l, :], [W.h_ssm_g])
                for h in range(HA if "wsk" not in DBG else 0):
                    xh = c.X // 2
                    self.ldc(W.wsk[:, h, 0:xh], d["c_wsk"][:, h, 0:xh], [W.h_wsk])
                    self.ldc(W.wsk[:, h, xh:c.X], d["c_wsk"][:, h, xh:c.X], [W.h_wsk])
                if "wsm" not in DBG:
                    self.ldc(W.wsm[:], d["c_wsm"], [W.h_wsm])
                if "woa" not in DBG:
                    self.ldc(W.woa[:], d["w_out"][l, 0:DA, :].rearrange("(h e) n -> e h n", e=64), [W.h_woa])
                worv = d["w_out"][l, DA:DM, :].rearrange("(c p) n -> p c n", p=128)
                for k3 in range(3 if "wor" not in DBG else 0):
                    self.ldc(W.wor[:, 2 * k3:2 * k3 + 2, :], worv[:, 2 * k3:2 * k3 + 2, :], [W.h_wor[k3]])

            def fresh_weight_handles():
                W.h_win = [[R.H() for _ in range(2)] for _ in range(8)]
                W.h_woa, W.h_wor, W.h_ssm_g, W.h_wsk, W.h_wsm = R.H(), [R.H(), R.H(), R.H()], R.H(), R.H(), R.H()

            with R.phase("mixp%d" % l) as ph:
                if "wl" not in DBG:
                    issue_weight_loads()
                else:
                    fresh_weight_handles()
                P = self.alloc_mixer_bufs(ph, prompt=True)
                nt = c.T // NQ if "mixp" in c.parts else 0
                epi = None
                for i in range(nt):
                    nxt = src[:, (i + 1) * NQ:(i + 2) * NQ] if i + 1 < nt else None
                    epi = self.mixer_tile(l, P, True, i, src[:, i * NQ:(i + 1) * NQ], dst[:, i * NQ:(i + 1) * NQ], NQ,
                                          preloaded=(i > 0), nxt=nxt, prev_epi=epi)
                if epi is not None:
                    for _ in epi:
                        pass
                if "fin" not in DBG:
                    self.st(d["phT"][l], P.hst[:], [P.h_hst])
                    self.st(d["pconv"][l].rearrange("(c p) t -> p c t", p=128), P.xbc[:, :, NQ:NQ + 3], P.h_xbc)
                    self.st(d["psconv"][l].rearrange("(c p) t -> p c t", p=128), P.cub[:, :, NQ:NQ + 2], P.h_cub)
            with R.phase("mixs%d" % l) as ph:
                fresh_weight_handles()
                P = self.alloc_mixer_bufs(ph, prompt=False)
                if "mixs" in c.parts:
                    for _ in self.mixer_tile(l, P, False, 0, ssrc[:, 0:c.NTS], sdst[:, 0:c.NTS], c.NTS):
                        pass

    def alloc_mixer_bufs(self, ph, prompt):
        c = self.cfg
        R = self.R
        NQ = c.NQ if prompt else c.NTS
        nseg = 1 if prompt else c.NS
        P = type("P", (), {})()
        P.prompt = prompt
        P.NB = NQ
        f32, bf = F32, BF16
        if prompt:
            P.NSLOT = min(c.T // 128, 18)
            P.KT = ph.sb("KT", [128, 2, P.NSLOT * 128], bf)
            P.Vst = ph.sb("Vst", [128, P.NSLOT, HA, HD + 1], bf)
            P.h_KT = R.Hs("KT", P.NSLOT)
            P.h_V = R.Hs("V", P.NSLOT)
        else:
            P.NSLOT = c.NKT
            P.KT = ph.sb("kallT", [128, 2, c.NKT * 128], bf)
            P.Vst = ph.sb("Vall", [128, c.NKT, HA, HD + 1], bf)
            P.vraw = ph.sb("vraw", [128, c.NKT - 1, DA], f32)
            P.vnew = ph.sb("vnew", [c.SL, c.NS, HA, HD], bf)
            P.h_KT = [R.H()] * 1
            P.h_V = [R.H()] * 1
            P.h_vraw, P.h_vnew = R.H(), R.Hs("vnew", c.NS)
        P.xt = ph.sb("xt", [128, 8, NQ], f32); P.h_xt = R.Hs("xt", 8)
        P.sq = ph.sb("sq", [128, 8, NQ], bf); P.h_sq = R.H()
        P.rs = ph.sb("rs", [128, NQ], f32); P.h_rs = R.H()
        P.rstd = ph.sb("rstd", [128, NQ], f32); P.h_rstd = R.H()
        P.hT = ph.sb("hT", [128, 8, NQ], bf); P.h_hT = R.Hs("hT", 8)
        P.QT = ph.sb("QT", [128, 2, NQ], bf); P.h_QT = R.Hs("QT", 2)
        P.kst = ph.sb("kst", [128, 2, NQ], f32); P.h_kst = R.Hs("kst", 2)
        P.xbc = ph.sb("xbc", [128, 8, NQ + 3 * nseg], f32); P.h_xbc = R.Hs("xbc", 8)
        P.xcx = ph.sb("xcx", [128, 4, NQ], f32); P.h_xcx = R.Hs("xcx", 4)
        P.xcb = ph.sb("xcb", [128, 4, NQ], bf); P.h_xcb = R.Hs("xcb", 4)
        P.cvt = ph.sb("cvt", [128, NQ], f32); P.h_cvt = R.H()
        P.cvs = ph.sb("cvs", [128, max(NQ, DB)], f32); P.h_cvs = R.H()
        P.cub = ph.sb("cub", [128, 2, NQ + 2 * nseg], f32); P.h_cub = R.Hs("cub", 2)
        P.bT = ph.sb("bT", [128, 2, NQ], f32); P.h_bT = R.Hs("bT", 2)
        P.yc = ph.sb("yc", [128, 2, NQ], f32); P.h_yc = R.Hs("yc", 2)
        P.csb = P.yc; P.h_csb = P.h_yc
        P.xr = None
        P.ycn = ph.sb("ycn", [128, 2, NQ], bf); P.h_ycn = R.Hs("ycn", 2)
        P.pexp = [ph.sb("pexp%d" % i, [128, max(NQ, 136)], bf) for i in range(3)]; P.h_pexp = R.Hs("pexp", 3)
        P.pw = [ph.sb("pw%d" % i, [128, max(NQ, 136)], bf) for i in range(4)]; P.h_pw = R.Hs("pw", 4)
        P.osb = ph.sb("osb", [128, NQ], f32); P.h_osb = R.H()
        P.cvt2 = ph.sb("cvt2", [128, NQ], f32); P.h_cvt2 = R.H()
        P.h_cvs2 = R.H()
        P.oan = ph.sb("oan", [64, HA, NQ], bf); P.h_oan = R.Hs("oan", 4)
        P.arA = ph.sb("arA", [128, 2048], f32); P.h_arA = R.Hs("arA", 4)
        P.arB = ph.sb("arB", [128, 2048], f32); P.h_arB = R.Hs("arB", 2)
        P.oa = P.arA[0:64, 1024:1024 + HA * NQ].rearrange("p (h n) -> p h n", h=HA)
        P.t1 = P.arA[:, 0:512]; P.t2 = P.arA[:, 512:1024]
        P.xtoks = [ph.sb("xtok%d" % i, [128, DB], f32) for i in range(2)]; P.h_xtoks = R.Hs("xtok", 2)
        P.Rm = P.arB[:, 0:1024].rearrange("p (h l) -> p h l", h=8)
        P.Dm = P.arB[:, 1024:2048].rearrange("p (h l) -> p h l", h=8)
        P.mix = P.arB[:, 0:8 * NQ].rearrange("p (c n) -> p c n", c=8)
        P.vst32 = P.kst[:, 1, :] if NQ >= DA else ph.sb("vst32", [128, DA], f32)
        P.h_vst32 = P.h_kst[1] if NQ >= DA else R.H()
        nch = 2 if prompt else c.NS
        P.sz = [ph.sb("sz%d" % i, [128, DB], f32) for i in range(nch)]; P.h_sz = R.Hs("sz", nch)
        P.dt = ph.sb("dt", [128, 4, 8], f32); P.h_dt = R.Hs("dt", 4)
        P.dta = ph.sb("dta", [128, 8], f32); P.h_dta = R.H()
        P.ea = ph.sb("ea", [128, 16], f32); P.h_ea = R.H()
        P.btoks = [ph.sb("btok%d" % i, [128, 256], bf) for i in range(2)]; P.h_btoks = R.Hs("btok", 2)
        P.GM = ph.sb("GM", [128, 2, 128], f32); P.h_GM = R.H()
        P.MT = ph.sb("MT", [128, 8, 128], bf); P.h_MT = R.H()
        P.xdt = ph.sb("xdt", [128, 8, HD], bf); P.h_xdt = R.H()
        P.xw = ph.sb("xw", [128, 8, HD], bf); P.h_xw = R.H()
        P.ss = ph.sb("ss", [128, 4], f32); P.h_ss = R.H()
        P.ybt = ph.sb("ybt", [128, DB], bf); P.h_ybt = R.H()
        P.ybT = ph.sb("ybT", [128, 4, NQ], bf); P.h_ybT = R.Hs("ybT", 4)
        P.hst = ph.sb("hst", [128, DB], f32); P.h_hst = R.H()
        P.hstb = ph.sb("hstb", [128, DB], bf); P.h_hstb = R.H()
        P.htmp = P.t2; P.h_htmp = P.h_arA[1]
        P.t3 = P.cvs[:, 0:DB]; P.h_t3 = [P.h_cvs, P.h_cvs2]
        P.psA = ph.ps("psA", [128, 512], f32); P.psB = ph.ps("psB", [128, 512], f32)
        P.psC = ph.ps("psC", [128, 512], f32); P.psD = ph.ps("psD", [128, 512], f32)
        P.psE = ph.ps("psE", [128, 512], f32); P.psF = ph.ps("psF", [128, 512], f32)
        P.psG = ph.ps("psG", [128, 512], f32); P.psH = ph.ps("psH", [128, 1024], bf)
        for n in "ABCDEFGH":
            setattr(P, "h_ps" + n, R.H("ps" + n, bank=True))
        P.h_psF2 = P.h_psF
        if "init" in DBG:
            return P
        self.memset("pool", P.Vst[:, :, :, HD:HD + 1], 1.0, P.h_V)
        if prompt:
            self.memset("pool", P.xbc[:, :, 0:3], 0.0, P.h_xbc)
            self.memset("pool", P.cub[:, :, 0:2], 0.0, P.h_cub)
        if prompt:
            self.memset("pool", P.hst[:], 0.0, [P.h_hst])
            self.memset("pool", P.hstb[:], 0.0, [P.h_hstb])
        else:
            self.memset("pool", P.KT[:, :, c.WB:c.NKT * 128], 0.0, P.h_KT)
            self.memset("pool", P.Vst[:, c.NKT - 1, :, 0:HD], 0.0, P.h_V)
        return P

    def mixer_tile(self, l, P, prompt, i, src, dst, N, preloaded=False, nxt=None, prev_epi=None):
        c = self.cfg
        R = self.R
        G = self.G
        W = self.W
        d = self.dram
        NQ = c.NQ
        T = c.T
        win = W.win
        hwin = W.h_win
        if prompt:
            q0 = i * NQ
            chunks = [(0, 128), (128, 128)]
            nseg, Ls = 1, NQ
        else:
            q0 = 0
            chunks = [(b * c.SL, c.SL) for b in range(c.NS)]
            nseg, Ls = c.NS, c.SL
        xbc4 = P.xbc[:, :, 0:nseg * (Ls + 3)].rearrange("p c (s t) -> p c s t", s=nseg)
        cub4 = P.cub[:, :, 0:nseg * (Ls + 2)].rearrange("p c (s t) -> p c s t", s=nseg)

        def seg3(ap2):
            return ap2.rearrange("p (s t) -> p s t", s=nseg)

        def load_norm(src_ap):
            self.ld(P.xt[:, :, 0:N], src_ap.rearrange("(c p) n -> p c n", p=128), P.h_xt)
            rstd_ = self.rms_stats(P, P.xt[:, :, 0:N], N, 8, 128, P.psF, P.h_xt, P.h_psF, DM)
            for cc in range(8):
                self.stt(P.hT[:, cc, 0:N], P.xt[:, cc, 0:N], G.gpre[:, l, cc:cc + 1], rstd_[:, 0:N], ALU.mult, ALU.mult,
                         [P.h_xt[cc], P.h_rstd], [P.h_hT[cc]])
        if not prompt:
            for b in range(c.NS):
                self.ld(xbc4[:, :, b, 0:3], d["conv0"][l, b].rearrange("(c p) t -> p c t", p=128), P.h_xbc)
                self.ld(cub4[:, :, b, 0:2], d["sconv0"][l, b].rearrange("(c p) t -> p c t", p=128), P.h_cub)
        if not preloaded:
            load_norm(src)


        pcnt = [0]
        steps = []

        def proj(col0, evac):
            steps.append(lambda: proj_now(col0, evac))

        PB6 = ((P.psA, P.h_psA), (P.psB, P.h_psB), (P.psC, P.h_psC), (P.psD, P.h_psD), (P.psE, P.h_psE), (P.psG, P.h_psG))

        def proj_now(col0, evac):
            ps, hps = PB6[pcnt[0] % 6]
            pcnt[0] += 1
            for kc in range(8):
                self.mm(ps[:, 0:N], win[:, kc, col0:col0 + 128], P.hT[:, kc, 0:N],
                        [hwin[kc][0 if col0 < 1280 else 1], P.h_hT[kc]], [hps], start=(kc == 0), stop=(kc == 7))
            evac(ps[:, 0:N], hps)

        def hk(j):
            return P.h_KT[j % len(P.h_KT)]

        def hv(j):
            return P.h_V[j % len(P.h_V)]

        need_kv_out = (not prompt) or (q0 >= T - c.WBP)
        for g in range(2 if "p_q" not in DBG else 0):
            proj(C_Q + 128 * g, lambda ps, hps, g=g: self.cp("act", P.QT[:, g, 0:N], ps, [hps], [P.h_QT[g]]))
        for g in range(2 if "p_k" not in DBG else 0):
            def ev_k(ps, hps, g=g):
                if prompt:
                    s0 = (2 * i) % P.NSLOT
                    self.cp("act", P.KT[:, g, s0 * 128:s0 * 128 + N], ps, [hps], [hk(2 * i), hk(2 * i + 1)])
                if need_kv_out:
                    self.cp("dve", P.kst[:, g, 0:N], ps, [hps], [P.h_kst[g]])
            proj(C_K + 128 * g, ev_k)
        def st_k():
            if prompt:
                o0 = q0 - (T - c.WBP)
                self.st(d["pkT"][l][:, o0:o0 + N].rearrange("(g p) n -> p g n", p=128), P.kst[:, :, 0:N], P.h_kst)
            else:
                for b in range(c.NS):
                    self.st(d["skT"][l, b][:, c.WB - c.SL:c.WB].rearrange("(g p) n -> p g n", p=128),
                            P.kst[:, :, b * c.SL:(b + 1) * c.SL], P.h_kst)
        if need_kv_out and "p_k" not in DBG:
            steps.append(st_k)
        for cc in range(8 if "p_xbc" not in DBG else 0):
            proj(C_XBC + 128 * cc, lambda ps, hps, cc=cc: self.cp(
                "dve", xbc4[:, cc, :, 3:3 + Ls], seg3(ps), [hps], [P.h_xbc[cc]]))
        for g in range(2 if "p_g" not in DBG else 0):
            proj(C_BG + 128 * g, lambda ps, hps, g=g: self.cp("act", P.bT[:, g, 0:N], ps, [hps], [P.h_bT[g]]))
        for g in range(2 if "p_g" not in DBG else 0):
            proj(C_CG + 128 * g, lambda ps, hps, g=g: self.cp("act", P.csb[:, g, 0:N], ps, [hps], [P.h_csb[g]]))
            proj(C_UC + 128 * g, lambda ps, hps, g=g: self.tt(
                "dve", cub4[:, g, :, 2:2 + Ls], seg3(P.csb[:, g, 0:N]), seg3(ps), ALU.mult,
                [hps, P.h_csb[g]], [P.h_cub[g]]))

        for st_ in steps:
            st_()
            if prev_epi is not None:
                try:
                    next(prev_epi)
                except StopIteration:
                    prev_epi = None
        if prev_epi is not None:
            for _ in prev_epi:
                pass

        def tm_v(ci, c0, L, psx, hpsx):
            for kc in range(8):
                self.mm(psx[0:L, 0:DA], P.hT[:, kc, c0:c0 + L], win[:, kc, C_V:C_V + DA],
                        [hwin[kc][0], P.h_hT[kc]], [hpsx], start=(kc == 0), stop=(kc == 7))
            psv4 = psx[0:L, 0:DA].rearrange("p (h e) -> p h e", h=HA)
            if prompt:
                j = 2 * i + ci
                self.cp("act", P.Vst[0:L, j % P.NSLOT, :, 0:HD], psv4, [hpsx], [hv(j)])
            else:
                self.cp("act", P.vnew[0:L, ci, :, :], psv4, [hpsx], [P.h_vnew[ci]])
            if need_kv_out:
                self.cp("dve", P.vst32[0:L, :], psx[0:L, 0:DA], [hpsx], [P.h_vst32])
                if prompt:
                    o0 = q0 + c0 - (T - c.WBP)
                    self.st(d["pv"][l, o0:o0 + L, :], P.vst32[0:L, :], [P.h_vst32])
                else:
                    self.st(d["sv"][l, ci, c.WB - c.SL:c.WB, :], P.vst32[0:L, :], [P.h_vst32])

        def tm_z(ci, c0, L):
            for kc in range(8):
                self.mm(P.psB[0:L, 0:DB], P.hT[:, kc, c0:c0 + L], win[:, kc, C_Z:C_Z + DB],
                        [hwin[kc][0], P.h_hT[kc]], [P.h_psB], start=(kc == 0), stop=(kc == 7))
            szb, hszb = P.sz[ci % len(P.sz)], P.h_sz[ci % len(P.sz)]
            self.act(szb[0:L, :], P.psB[0:L, 0:DB], AF.Exp, [P.h_psB], [hszb], scale=-1.0)
            self.act(szb[0:L, :], szb[0:L, :], AF.Ln, [hszb], [hszb], bias=1.0)
            self.act(szb[0:L, :], szb[0:L, :], AF.Exp, [hszb], [hszb], scale=-1.0)
            self.tt("dve", szb[0:L, :], szb[0:L, :], P.psB[0:L, 0:DB], ALU.mult, [hszb, P.h_psB], [hszb])

        def tm_dt(ci, c0, L):
            for kc in range(8):
                self.mm(P.psF[0:L, 0:8], P.hT[:, kc, c0:c0 + L], win[:, kc, C_DT:C_DT + 8],
                        [hwin[kc][1], P.h_hT[kc]], [P.h_psF], start=(kc == 0), stop=(kc == 7))
            self.tt("dve", P.dt[0:L, ci, :], P.psF[0:L, 0:8], G.dtb[0:L, l, :], ALU.add, [P.h_psF], [P.h_dt[ci]])
            self.act(P.dt[0:L, ci, :], P.dt[0:L, ci, :], AF.Exp, [P.h_dt[ci]], [P.h_dt[ci]])
            self.act(P.dt[0:L, ci, :], P.dt[0:L, ci, :], AF.Ln, [P.h_dt[ci]], [P.h_dt[ci]], bias=1.0)

        if not prompt:
            for ci, (c0, L) in enumerate(chunks):
                tm_v(ci, c0, L, P.psA, P.h_psA)
                tm_z(ci, c0, L)
                tm_dt(ci, c0, L)

        h_oa = [P.h_arA[2 + (h * P.NB) // 512] for h in range(HA)]
        NQB = P.NB

        def g_attention():
            if prompt:
                for ci, (c0, L) in enumerate(chunks):
                    tm_v(ci, c0, L, P.psC, P.h_psC)
                    yield
                jmin = max(0, (q0 - 2048) // 128)
                jmax = 2 * i + 1
                js = list(range(jmin, jmax + 1))
                nj = len(js)
                for h in range(HA):
                    half, g = (h % 2) * 64, h // 2

                    SB = ((P.psC, P.h_psC), (P.psD, P.h_psD), (P.psG, P.h_psG))

                    def S(jj):
                        j = js[jj]
                        ps, hps = SB[jj % 3]
                        s0 = (j % P.NSLOT) * 128
                        self.mm(ps[:, 0:N], P.KT[half:half + 64, g, s0:s0 + 128], P.QT[half:half + 64, g, 0:N],
                                [hk(j), P.h_QT[g]], [hps])

                    def E(jj):
                        ps, hps = SB[jj % 3]
                        self.act(P.pexp[jj % 3][:, 0:N], ps[:, 0:N], AF.Exp, [hps], [P.h_pexp[jj % 3]], scale=0.125)

                    def M(jj):
                        j = js[jj]
                        delta = q0 - 128 * j
                        self.tt("pool" if jj % 3 == 2 else "dve", P.pw[jj % 4][:, 0:N], P.pexp[jj % 3][:, 0:N],
                                W.wsk[:, h, WOFF + delta:WOFF + delta + N], ALU.mult,
                                [P.h_pexp[jj % 3], W.h_wsk], [P.h_pw[jj % 4]])

                    def V(jj):
                        j = js[jj]
                        self.mm(P.psE[0:HD + 1, 0:N], P.Vst[:, j % P.NSLOT, h, 0:HD + 1], P.pw[jj % 4][:, 0:N],
                                [hv(j), P.h_pw[jj % 4]], [P.h_psE], start=(jj == 0), stop=(jj == nj - 1))
                    S(0)
                    for it in range(nj + 4):
                        if it + 1 < nj:
                            S(it + 1)
                        if it < nj:
                            E(it)
                        if 0 <= it - 1 < nj:
                            M(it - 1)
                        if 0 <= it - 4 < nj:
                            V(it - 4)
                        yield
                    self.attn_norm_head(P, h, N, P.psE[0:HD + 1, 0:N], h_oa)
                    yield
            else:
                for b in range(c.NS):
                    for g in range(2):
                        self.ldc(P.KT[:, g, 0:c.WB], d["ckT"][l, b][g * 128:(g + 1) * 128, :], [P.h_KT[0]])
                    self.ld(P.vraw[:], d["cv"][l, b].rearrange("(j p) f -> p j f", p=128), [P.h_vraw])
                    for g in range(2):
                        self.cp("act", P.KT[:, g, c.WB:c.WB + c.SL], P.kst[:, g, b * c.SL:(b + 1) * c.SL],
                                [P.h_kst[g]], [P.h_KT[0]])
                    self.cp("pool", P.Vst[:, 0:c.NKT - 1, :, 0:HD], P.vraw[:].rearrange("p j (h e) -> p j h e", h=HA),
                            [P.h_vraw], [P.h_V[0]])
                    self.cp("pool", P.Vst[0:c.SL, c.NKT - 1, :, 0:HD], P.vnew[0:c.SL, b, :, :], [P.h_vnew[b]], [P.h_V[0]])
                    yield
                    for h in range(HA):
                        half, g = (h % 2) * 64, h // 2
                        ps, hps = ((P.psC, P.h_psC), (P.psD, P.h_psD))[h % 2]
                        nk = c.NKT
                        for j in range(nk):
                            self.mm(ps[:, j * c.SL:(j + 1) * c.SL], P.KT[half:half + 64, g, j * 128:(j + 1) * 128],
                                    P.QT[half:half + 64, g, b * c.SL:(b + 1) * c.SL], [P.h_KT[0], P.h_QT[g]], [hps])
                        self.act(P.pexp[h % 2][:, 0:nk * c.SL], ps[:, 0:nk * c.SL], AF.Exp, [hps], [P.h_pexp[h % 2]], scale=0.125)
                        self.tt("dve", P.pw[h % 2][:, 0:nk * c.SL], P.pexp[h % 2][:, 0:nk * c.SL],
                                W.wsm[:, h, :, :].rearrange("p j s -> p (j s)"), ALU.mult,
                                [P.h_pexp[h % 2], W.h_wsm], [P.h_pw[h % 2]])
                        o0 = h * c.NTS + b * c.SL
                        for j in range(nk):
                            self.mm(P.psE[0:HD + 1, o0:o0 + c.SL], P.Vst[:, j, h, 0:HD + 1],
                                    P.pw[h % 2][:, j * c.SL:(j + 1) * c.SL],
                                    [P.h_V[0], P.h_pw[h % 2]], [P.h_psE], start=(j == 0), stop=(j == nk - 1))
                        yield
                for h in range(HA):
                    self.attn_norm_head(P, h, N, P.psE[0:HD + 1, h * c.NTS:h * c.NTS + N], h_oa)
                    yield
            rstd = self.rms_stats(P, P.oa[0:64, :, 0:N], N, HA, 64, P.psF, list(set(h_oa)), P.h_psF, DA)
            for h in range(HA):
                self.stt(P.oan[0:64, h, 0:N], P.oa[0:64, h, 0:N], G.attn_g[0:64, l, h:h + 1], rstd[0:64, 0:N],
                         ALU.mult, ALU.mult, [h_oa[h], P.h_rstd], [P.h_oan[h]])
            yield

        def g_zdt():
            for ci, (c0, L) in enumerate(chunks):
                tm_z(ci, c0, L)
                tm_dt(ci, c0, L)
                yield

        def g_ssm():
            cvts = ((P.cvt, P.h_cvt, P.cvs[:, 0:NQB], P.h_cvs), (P.cvt2, P.h_cvt2, P.cvs[:, NQB:2 * NQB], P.h_cvs2))

            def conv_a(cc):
                cvt, hcvt, cvs, hcvs = cvts[cc % 2]
                c3 = seg3(cvt[:, 0:N])
                self.ts("dve", c3, xbc4[:, cc, :, 0:Ls], G.conv_w[:, l, cc, 0:1], ALU.mult, [P.h_xbc[cc]], [hcvt])
                for k in range(1, 4):
                    self.stt(c3, xbc4[:, cc, :, k:k + Ls], G.conv_w[:, l, cc, k:k + 1], c3, ALU.mult, ALU.add,
                             [P.h_xbc[cc], hcvt], [hcvt])
                self.act(cvs[:, 0:N], cvt[:, 0:N], AF.Exp, [hcvt], [hcvs], scale=-1.0, bias=G.nconv_b[:, l, cc:cc + 1])
                self.act(cvs[:, 0:N], cvs[:, 0:N], AF.Ln, [hcvs], [hcvs], bias=1.0)
                self.act(cvs[:, 0:N], cvs[:, 0:N], AF.Exp, [hcvs], [hcvs], scale=-1.0)

            def conv_b(cc):
                cvt, hcvt, cvs, hcvs = cvts[cc % 2]
                if cc < 4:
                    self.stt(P.xcx[:, cc, 0:N], cvt[:, 0:N], G.conv_b[:, l, cc:cc + 1], cvs[:, 0:N], ALU.add, ALU.mult,
                             [hcvt, hcvs], [P.h_xcx[cc]])
                else:
                    self.stt(P.xcb[:, cc - 4, 0:N], cvt[:, 0:N], G.conv_b[:, l, cc:cc + 1], cvs[:, 0:N], ALU.add, ALU.mult,
                             [hcvt, hcvs], [P.h_xcb[cc - 4]])
            early = prompt and not c.serial
            if early:
                g0 = self.ssd_chunk(l, P, prompt, 0, chunks[0][0], chunks[0][1])
                g1 = self.ssd_chunk(l, P, prompt, 1, chunks[1][0], chunks[1][1])
            conv_a(0)
            yield
            for cc in range(1, 8):
                conv_a(cc)
                conv_b(cc - 1)
                if early and cc == 6:
                    next(g0)
                yield
            conv_b(7)
            if prompt:
                self.cp("pool", P.xbc[:, :, 0:3], P.xbc[:, :, N:N + 3], P.h_xbc, P.h_xbc)
            else:
                for b in range(c.NS):
                    self.st(d["sconv"][l, b].rearrange("(c p) t -> p c t", p=128), xbc4[:, :, b, Ls:Ls + 3], P.h_xbc)
            if early:
                next(g0)
            yield
            if early:
                for _ in range(3):
                    next(g0)
                    yield
                live0 = True
                while True:
                    if live0:
                        try:
                            next(g0)
                        except StopIteration:
                            live0 = False
                    try:
                        next(g1)
                    except StopIteration:
                        break
                    yield
            else:
                for ci, (c0, L) in enumerate(chunks):
                    yield from self.ssd_chunk(l, P, prompt, ci, c0, L)

        def g_sconv():
            for g in range(2):
                self.ts("pool", P.yc[:, g, 0:N].rearrange("p (s t) -> p s t", s=nseg), cub4[:, g, :, 0:Ls],
                        G.sconv_w[:, l, g, 0:1], ALU.mult, [P.h_cub[g]], [P.h_yc[g]])
                yield
                yc3 = P.yc[:, g, 0:N].rearrange("p (s t) -> p s t", s=nseg)
                for k in range(1, 3):
                    self.stt(yc3, cub4[:, g, :, k:k + Ls], G.sconv_w[:, l, g, k:k + 1], yc3, ALU.mult, ALU.add,
                             [P.h_cub[g], P.h_yc[g]], [P.h_yc[g]])
                    yield
                self.tt("pool", P.yc[:, g, 0:N], P.yc[:, g, 0:N], P.bT[:, g, 0:N], ALU.mult, [P.h_yc[g], P.h_bT[g]], [P.h_yc[g]])
                yield
            if prompt:
                self.cp("pool", P.cub[:, :, 0:2], P.cub[:, :, N:N + 2], P.h_cub, P.h_cub)
            else:
                for b in range(c.NS):
                    self.st(d["ssconv"][l, b].rearrange("(c p) t -> p c t", p=128), cub4[:, :, b, Ls:Ls + 2], P.h_cub)
            rstd = self.rms_stats(P, P.yc[:, :, 0:N], N, 2, 128, P.psF, P.h_yc, P.h_psF, DC)
            for g in range(2):
                self.stt(P.ycn[:, g, 0:N], P.yc[:, g, 0:N], G.sconv_g[:, l, g:g + 1], rstd[:, 0:N], ALU.mult, ALU.mult,
                         [P.h_yc[g], P.h_rstd], [P.h_ycn[g]])
            yield

        def g_prefetch():
            for _ in range(10):
                yield
            load_norm(nxt)
            yield

        if c.serial:
            for gen in ((g_zdt(),) if prompt else ()) + (g_attention(), g_ssm(), g_sconv()) + ((g_prefetch(),) if nxt is not None else ()):
                for _ in gen:
                    pass
        else:
            gens = [[g_attention(), c.w_att], [g_ssm(), 1], [g_sconv(), 1]] + ([[g_zdt(), 1]] if prompt else []) \
                + ([[g_prefetch(), 1]] if nxt is not None else [])
            while gens:
                for ent in list(gens):
                    try:
                        for _ in range(ent[1]):
                            next(ent[0])
                    except StopIteration:
                        gens.remove(ent)

        h_mix = P.h_arB
        for dc in range(8):
            ps, hps = PB6[dc % 6]
            cs = slice(dc * 128, (dc + 1) * 128)
            for h in range(HA):
                self.mm(ps[:, 0:N], W.woa[0:64, h, cs], P.oan[0:64, h, 0:N], [W.h_woa, P.h_oan[h]], [hps],
                        start=(h == 0), stop=False)
            for cc in range(4):
                self.mm(ps[:, 0:N], W.wor[:, cc, cs], P.ybT[:, cc, 0:N], [W.h_wor[cc // 2], P.h_ybT[cc]], [hps],
                        start=False, stop=False)
            for g in range(2):
                self.mm(ps[:, 0:N], W.wor[:, 4 + g, cs], P.ycn[:, g, 0:N], [W.h_wor[2], P.h_ycn[g]], [hps],
                        start=False, stop=(g == 1))
            self.cp("act", P.mix[:, dc, 0:N], ps[:, 0:N], [hps], h_mix)
        def g_epi():
            self.ld(P.xt[:, :, 0:N], src.rearrange("(c p) n -> p c n", p=128), P.h_xt)
            rstd = self.rms_stats(P, P.mix[:, :, 0:N], N, 8, 128, P.psF, h_mix, P.h_psF, DM)
            yield
            for cc in range(8):
                self.stt(P.mix[:, cc, 0:N], P.mix[:, cc, 0:N], G.gpost[:, l, cc:cc + 1], rstd[:, 0:N], ALU.mult, ALU.mult,
                         h_mix + [P.h_rstd], h_mix)
                self.tt("pool", P.xt[:, cc, 0:N], P.mix[:, cc, 0:N], P.xt[:, cc, 0:N], ALU.add,
                        h_mix + [P.h_xt[cc]], [P.h_xt[cc]])
                yield
            self.st(dst.rearrange("(c p) n -> p c n", p=128), P.xt[:, :, 0:N], P.h_xt)
            yield
        return g_epi()

    def attn_norm_head(self, P, h, N, pso, h_oa):
        G = self.G
        self.cp("act", P.osb[0:HD + 1, 0:N], pso, [P.h_psE], [P.h_osb])
        self.act(P.osb[64:65, 0:N], P.osb[64:65, 0:N], AF.Ln, [P.h_osb], [P.h_osb])
        self.act(P.osb[64:65, 0:N], P.osb[64:65, 0:N], AF.Exp, [P.h_osb], [P.h_osb], scale=-1.0)
        self.mm(P.psF[0:64, 0:N], G.onesf[64:65, 0:64], P.osb[64:65, 0:N], [P.h_osb], [P.h_psF])
        self.tt("dve", P.oa[0:64, h, 0:N], P.osb[0:64, 0:N], P.psF[0:64, 0:N], ALU.mult, [P.h_osb, P.h_psF], [h_oa[h]])

    def ssd_chunk(self, l, P, prompt, ci, c0, L):
        c = self.cfg
        R = self.R
        G = self.G
        W = self.W
        d = self.dram
        hA = P.h_arA
        h_t1, h_t2, h_t3 = hA[0], hA[1], P.h_t3
        xtok, h_xtok = P.xtoks[ci % 2], P.h_xtoks[ci % 2]
        btok, h_btok = P.btoks[ci % 2], P.h_btoks[ci % 2]
        h_Rm, h_Dm = P.h_arB[0], P.h_arB[1]
        L4 = 4 * L
        if not prompt:
            self.ld(P.hst[:], d["h0T"][l, ci], [P.h_hst])
            self.cp("pool", P.hstb[:], P.hst[:], [P.h_hst], [P.h_hstb])
        dt = P.dt[0:L, ci, :]
        xt3 = xtok[0:L, :].rearrange("p (h e) -> p h e", h=HB)
        for cc in range(4):
            self.tr(P.psF[0:L, cc * 128:(cc + 1) * 128], P.xcx[:, cc, c0:c0 + L], G.identf[:, :], [P.h_xcx[cc]], [P.h_psF])
        self.cp("act", xtok[0:L, :], P.psF[0:L, 0:DB], [P.h_psF], [h_xtok])
        for g in range(2):
            self.tr(P.psH[0:L, g * 128:(g + 1) * 128], P.xcb[:, g, c0:c0 + L], G.identb[:, :], [P.h_xcb[g]], [P.h_psH])
        self.cp("dve", btok[0:L, :], P.psH[0:L, 0:256], [P.h_psH], [h_btok])
        self.tt("dve", P.dta[0:L, :], dt, G.aneg[0:L, l, :], ALU.mult, [P.h_dt[ci]], [P.h_dta])
        self.tt("dve", P.Rm[0:L, :, 0:L], G.U[0:L, 0:L].unsqueeze(1).broadcast_to([L, 8, L]),
                P.dta[0:L, :].unsqueeze(2).broadcast_to([L, 8, L]), ALU.mult, [P.h_dta], [h_Rm])
        yield
        for hh in range(2):
            self.mm(P.psB[0:L, 0:L4].rearrange("p (h l) -> p h l", h=4), G.Lst[0:L, 0:L], P.Rm[0:L, 4 * hh:4 * hh + 4, 0:L],
                    [h_Rm], [P.h_psB])
            self.act(P.Dm[0:L, 4 * hh:4 * hh + 4, 0:L], P.psB[0:L, 0:L4].rearrange("p (h l) -> p h l", h=4), AF.Exp,
                     [P.h_psB], [h_Dm])
            if hh == 0:
                for g in range(2):
                    self.mm(P.psF[0:L, g * 128:g * 128 + L], P.xcb[:, g, c0:c0 + L], P.xcb[:, 2 + g, c0:c0 + L],
                            [P.h_xcb[g], P.h_xcb[2 + g]], [P.h_psF])
                self.tt("dve", P.GM[0:L, :, 0:L], P.psF[0:L, 0:256].rearrange("p (g l) -> p g l", g=2)[:, :, 0:L],
                        G.U[0:L, 0:L].unsqueeze(1).broadcast_to([L, 2, L]), ALU.mult, [P.h_psF], [P.h_GM])
                self.tt("pool", P.xdt[0:L, :, :], xt3, dt.unsqueeze(2).broadcast_to([L, HB, HD]), ALU.mult,
                        [h_xtok, P.h_dt[ci]], [P.h_xdt])
            else:
                self.mm(P.psF[0:L, 400:408], G.U[0:L, 0:L], P.dta[0:L, :], [P.h_dta], [P.h_psF])
                self.mm(P.psF[0:128, 408:416], G.onesf[0:L, 0:128], P.dta[0:L, :], [P.h_dta], [P.h_psF])
                self.act(P.ea[0:L, 0:8], P.psF[0:L, 400:408], AF.Exp, [P.h_psF], [P.h_ea])
                self.act(P.ea[:, 8:16], P.psF[:, 408:416], AF.Exp, [P.h_psF], [P.h_ea])
            yield
        for g in range(2):
            self.tt("dve", P.MT[0:L, 4 * g:4 * g + 4, 0:L], P.Dm[0:L, 4 * g:4 * g + 4, 0:L],
                    P.GM[0:L, g:g + 1, 0:L].broadcast_to([L, 4, L]), ALU.mult, [h_Dm, P.h_GM], [P.h_MT])
        self.tt("dve", P.xw[0:L, :, :], P.xdt[0:L, :, :], P.Dm[0:L, :, L - 1:L].broadcast_to([L, HB, HD]), ALU.mult,
                [P.h_xdt, h_Dm], [P.h_xw])
        yield
        for h in range(HB):
            self.mm(P.psA[0:L, h * HD:(h + 1) * HD], P.MT[0:L, h, 0:L], P.xdt[0:L, h, :], [P.h_MT, P.h_xdt], [P.h_psA])
        for g in range(2):
            self.mm(P.psB[0:L, g * 256:(g + 1) * 256], P.xcb[:, 2 + g, c0:c0 + L], P.hstb[:, g * 256:(g + 1) * 256],
                    [P.h_xcb[2 + g], P.h_hstb], [P.h_psB])
        yield
        for g in range(2):
            self.mm(P.psF[:, g * 256:(g + 1) * 256], btok[0:L, g * 128:(g + 1) * 128],
                    P.xw[0:L, 4 * g:4 * g + 4, :].rearrange("p h e -> p (h e)"), [h_btok, P.h_xw], [P.h_psF])
        self.tt("pool", P.htmp[:].rearrange("p (h e) -> p h e", h=HB), P.hst[:].rearrange("p (h e) -> p h e", h=HB),
                P.ea[:, 8:16].unsqueeze(2).broadcast_to([128, HB, HD]), ALU.mult, [P.h_hst, P.h_ea], [P.h_htmp])
        self.tt("dve", P.hst[:], P.htmp[:], P.psF[:, 0:DB], ALU.add, [P.h_htmp, P.h_psF], [P.h_hst])
        self.cp("pool", P.hstb[:], P.hst[:], [P.h_hst], [P.h_hstb])
        if not prompt:
            self.st(d["shT"][l, ci], P.hst[:], [P.h_hst])
        t1_3 = P.t1[0:L, :].rearrange("p (h e) -> p h e", h=HB)
        self.tt("dve", t1_3, P.psB[0:L, 0:DB].rearrange("p (h e) -> p h e", h=HB),
                P.ea[0:L, 0:8].unsqueeze(2).broadcast_to([L, HB, HD]), ALU.mult, [P.h_psB, P.h_ea], [h_t1])
        self.tt("dve", P.t1[0:L, :], P.t1[0:L, :], P.psA[0:L, 0:DB], ALU.add, [h_t1, P.h_psA], [h_t1])
        yield
        szi = P.sz[ci % len(P.sz)]
        hszi = P.h_sz[ci % len(P.sz)]
        t3_3 = P.t3[0:L, :].rearrange("p (h e) -> p h e", h=HB)
        self.tt("dve", t3_3, xt3, G.dsk[0:L, l, :].unsqueeze(2).broadcast_to([L, HB, HD]), ALU.mult, [h_xtok], h_t3)
        self.tt("dve", P.t1[0:L, :], P.t1[0:L, :], P.t3[0:L, :], ALU.add, [h_t1] + h_t3, [h_t1])
        self.tt("dve", P.t1[0:L, :], P.t1[0:L, :], szi[0:L, :], ALU.mult, [h_t1, hszi], [h_t1])
        yield
        self.act(P.t3[0:L, :], P.t1[0:L, :], AF.Square, [h_t1], h_t3 + [P.h_ss], accum=P.ss[0:L, 0:1])
        self.act(P.ss[0:L, 1:2], P.ss[0:L, 0:1], AF.Ln, [P.h_ss], [P.h_ss], scale=1.0 / DB, bias=G.eps[0:L, 0:1])
        self.act(P.ss[0:L, 2:3], P.ss[0:L, 1:2], AF.Exp, [P.h_ss], [P.h_ss], scale=-0.5)
        self.stt(P.ybt[0:L, :], P.t1[0:L, :], P.ss[0:L, 2:3], W.ssm_g[0:L, :], ALU.mult, ALU.mult,
                 [h_t1, P.h_ss, W.h_ssm_g], [P.h_ybt])
        yield
        for cc in range(4):
            self.tr(P.psH[:, 256 + cc * 128:256 + cc * 128 + L], P.ybt[0:L, cc * 128:(cc + 1) * 128], G.identb[0:L, 0:L],
                    [P.h_ybt], [P.h_psH])
        self.cp("act", P.ybT[:, :, c0:c0 + L], P.psH[:, 256:768].rearrange("p (c l) -> p c l", c=4)[:, :, 0:L],
                [P.h_psH], P.h_ybT)
        yield


def _pvec(v, D, nchunk):
    return np.ascontiguousarray(np.asarray(v, np.float32).reshape(D, nchunk, 128).transpose(2, 0, 1))


def _bvec(v, D):
    v = np.asarray(v, np.float32)
    return np.ascontiguousarray(np.broadcast_to(v[None], (128,) + v.shape))


def make_in_maps(inp, cfg, ncores, nseq):
    D, NS, WB, T = cfg.D, cfg.NS, cfg.WB, cfg.T
    f = lambda a: np.ascontiguousarray(np.asarray(a, np.float32))
    shared = {
        "w_in": f(inp["w_in"]), "w_out": f(inp["w_out"]), "w_up": f(inp["w_mlp_up"]), "w_dn": f(inp["w_mlp_down"]),
        "gpre": _pvec(inp["norm_mix_pre"], D, 8), "gpost": _pvec(inp["norm_mix_post"], D, 8),
        "gmpre": _pvec(inp["norm_mlp_pre"], D, 8), "gmpost": _pvec(inp["norm_mlp_post"], D, 8),
        "attn_g": f(np.asarray(inp["attn_norm"], np.float32).reshape(D, 4, 64).transpose(2, 0, 1)),
        "conv_w": f(np.asarray(inp["ssm_conv_w"], np.float32).reshape(D, 4, 8, 128).transpose(3, 0, 2, 1)),
        "conv_b": _pvec(inp["ssm_conv_b"], D, 8),
        "sconv_w": f(np.asarray(inp["sconv_w"], np.float32).reshape(D, 3, 2, 128).transpose(3, 0, 2, 1)),
        "sconv_g": _pvec(inp["sconv_norm"], D, 2),
        "dtb": _bvec(inp["ssm_dt_bias"], D), "alog": _bvec(inp["ssm_a_log"], D), "dsk": _bvec(inp["ssm_d"], D),
        "ssm_g": _bvec(inp["ssm_norm"], D),
    }
    shared.update(host_consts(cfg))
    xp = np.asarray(inp["x_prompt"], np.float32)
    xs = np.asarray(inp["x_sample"], np.float32)
    ck = np.asarray(inp["cache_attn_k"], np.float32)
    cv = np.asarray(inp["cache_attn_v"], np.float32)
    hs = np.asarray(inp["state_ssm"], np.float32)
    cs = np.asarray(inp["state_ssm_conv"], np.float32)
    ss = np.asarray(inp["state_sconv"], np.float32)
    maps = []
    for r in range(ncores):
        seq = r % nseq
        sl = slice(NS * r, NS * r + NS)
        m = dict(shared)
        m["xT"] = f(xp[seq].T)
        m["xsT"] = f(xs[sl].reshape(NS * cfg.SL, DM).T)
        m["ckT"] = f(ck[:, sl].reshape(D, NS, WB, DA).transpose(0, 1, 3, 2))
        m["cv"] = f(cv[:, sl].reshape(D, NS, WB, DA))
        m["h0T"] = f(hs[:, sl].reshape(D, NS, DB, NST).transpose(0, 1, 3, 2))
        m["conv0"] = f(cs[:, sl].transpose(0, 1, 3, 2))
        m["sconv0"] = f(ss[:, sl].transpose(0, 1, 3, 2))
        maps.append(m)
    return maps


def assemble(results, cfg, ncores, nseq, nsample):
    D, NS, WB, T, WBP = cfg.D, cfg.NS, cfg.WB, cfg.T, cfg.WBP
    yp = np.zeros((nseq, T, DM), np.float32)
    ys = np.zeros((nsample, cfg.SL, DM), np.float32)
    pk = np.zeros((D, nseq, WBP, HA, HD), np.float32)
    pv = np.zeros_like(pk)
    pssm = np.zeros((D, nseq, HB, HD, NST), np.float32)
    pconv = np.zeros((D, nseq, 3, DM), np.float32)
    psconv = np.zeros((D, nseq, 2, DC), np.float32)
    sk = np.zeros((D, nsample, WB, HA, HD), np.float32)
    sv = np.zeros_like(sk)
    sssm = np.zeros((D, nsample, HB, HD, NST), np.float32)
    sconv = np.zeros((D, nsample, 3, DM), np.float32)
    ssconv = np.zeros((D, nsample, 2, DC), np.float32)
    for r in range(ncores):
        o = results[r]
        sl = slice(NS * r, NS * r + NS)
        if r < nseq:
            yp[r] = o["yT"].T
            pk[:, r] = o["pkT"].transpose(0, 2, 1).reshape(D, WBP, HA, HD)
            pv[:, r] = o["pv"].reshape(D, WBP, HA, HD)
            pssm[:, r] = o["phT"].transpose(0, 2, 1).reshape(D, HB, HD, NST)
            pconv[:, r] = o["pconv"].transpose(0, 2, 1)
            psconv[:, r] = o["psconv"].transpose(0, 2, 1)
        ys[sl] = o["ysT"].T.reshape(NS, cfg.SL, DM)
        sk[:, sl] = o["skT"].transpose(0, 1, 3, 2).reshape(D, NS, WB, HA, HD)
        sv[:, sl] = o["sv"].reshape(D, NS, WB, HA, HD)
        sssm[:, sl] = o["shT"].transpose(0, 1, 3, 2).reshape(D, NS, HB, HD, NST)
        sconv[:, sl] = o["sconv"].transpose(0, 1, 3, 2)
        ssconv[:, sl] = o["ssconv"].transpose(0, 1, 3, 2)
    return (yp, ys, pk, pv, pssm, pconv, psconv, sk, sv, sssm, sconv, ssconv)


def run(inp, cfg, ncores, nseq):
    prog = Prog(cfg)
    nc = prog.build()
    maps = make_in_maps(inp, cfg, ncores, nseq)
    res = run_bass_kernel_spmd(nc, maps, core_ids=list(range(ncores)))
    return assemble(res.results, cfg, ncores, nseq, ncores * cfg.NS)


def kernel(**inputs):
    cfg = Cfg(T=4096, D=4, NS=4)
    return run(inputs, cfg, 8, 4)
```
